# Optimizing a Trainium2 kernel written in Bass

```python
import math
import jax, jax.numpy as jnp
from jax import lax
import numpy as np

D_MODEL = 1024
BATCH = 8
SEQ = 2048
DEPTH = 4

CHUNK = 64
N_A = DEPTH // 2
N_B = DEPTH - N_A
D_FF = 2816
GMLP_WIDTH = 4 * D_MODEL
GMLP_HALF = GMLP_WIDTH // 2
GMLP_WINDOW = 128
GMLP_GROUPS = 8
GMLP_GROUP_DIM = GMLP_HALF // GMLP_GROUPS
N_HEADS = 16
HEAD_DIM = D_MODEL // N_HEADS
LEFT_CHUNKS = 8
BAND = (LEFT_CHUNKS + 1) * CHUNK
LEFT_PAD = LEFT_CHUNKS * CHUNK
MAX_REL = 4 * CHUNK
N_REL = (CHUNK - 1) + MAX_REL + 1
ALPHA = (2.0 * DEPTH) ** 0.25
BETA = (8.0 * DEPTH) ** -0.25
LN_EPS = 1e-5
N_MOD = 9

kernel_name = "hybrid_gmlp_yoco_chunk_attention_encoder"


def layer_norm(x, g, b):
    xf = x.astype(jnp.float32)
    mu = jnp.mean(xf, axis=-1, keepdims=True)
    var = jnp.mean(jnp.square(xf - mu), axis=-1, keepdims=True)
    return ((xf - mu) * lax.rsqrt(var + LN_EPS)).astype(x.dtype) * g + b


def swiglu(h, w_gu, w_down):
    gu = h @ w_gu
    g, u = jnp.split(gu, 2, axis=-1)
    return (jax.nn.silu(g) * u) @ w_down


def gmlp_mixer(h, w_in, b_in, ln_g, ln_b, w_s, b_s, w_out):
    B, S, _ = h.shape
    z = jax.nn.gelu(h @ w_in + b_in, approximate=False)
    u, v = jnp.split(z, 2, axis=-1)
    v = layer_norm(v, ln_g, ln_b)
    v = v.reshape(B, S // GMLP_WINDOW, GMLP_WINDOW, GMLP_GROUPS, GMLP_GROUP_DIM)
    t = np.arange(GMLP_WINDOW)
    mask = ((t[None, :] // CHUNK) <= (t[:, None] // CHUNK)).astype(np.float32)
    ws = w_s * jnp.asarray(mask, dtype=w_s.dtype)[None]
    s = jnp.einsum('gts,bnsgc->bntgc', ws, v) + b_s.T[None, None, :, :, None]
    return (u * s.reshape(B, S, GMLP_HALF)) @ w_out


def chunk_band_attention(h, w_q, rel_bias, w_o, k_pad, v_pad):
    B, S, _ = h.shape
    n_chunks = S // CHUNK
    q = (h @ w_q).reshape(B, n_chunks, CHUNK, N_HEADS, HEAD_DIM)
    q = jnp.transpose(q, (1, 0, 2, 3, 4))
    t = np.arange(CHUNK)
    r = np.arange(BAND)
    dist = t[:, None] + LEFT_PAD - r[None, :]
    idx = np.clip(dist, -(CHUNK - 1), MAX_REL) + (CHUNK - 1)
    bias = rel_bias[:, idx].astype(jnp.float32)
    scale = HEAD_DIM ** -0.5
    r_j = jnp.arange(BAND, dtype=jnp.int32)

    def one_chunk(args):
        n, qn = args
        start = n * CHUNK
        kn = lax.dynamic_slice_in_dim(k_pad, start, BAND, axis=1)
        vn = lax.dynamic_slice_in_dim(v_pad, start, BAND, axis=1)
        valid = (start - LEFT_PAD + r_j) >= 0
        sc = jnp.einsum('bthd,brhd->bhtr', qn, kn).astype(jnp.float32) * scale + bias
        sc = jnp.where(valid[None, None, None, :], sc, -jnp.inf)
        p = jax.nn.softmax(sc, axis=-1).astype(vn.dtype)
        return jnp.einsum('bhtr,brhd->bthd', p, vn)

    out = lax.map(one_chunk, (jnp.arange(n_chunks, dtype=jnp.int32), q))
    out = jnp.transpose(out, (1, 0, 2, 3, 4)).reshape(B, S, D_MODEL)
    return out @ w_o


def setup_inputs(seed: int = 0) -> dict:
    key = jax.random.key(seed)
    ks = jax.random.split(key, 24)
    D = D_MODEL
    f32 = jnp.float32
    nrm = lambda k, shape: jax.random.normal(k, shape, dtype=f32)
    v_scale = jnp.concatenate([jnp.ones((D,), f32), jnp.full((D,), BETA, f32)])
    return {
        "x": nrm(ks[0], (BATCH, SEQ, D)),
        "c": nrm(ks[1], (BATCH, D)),
        "w_ada": nrm(ks[2], (DEPTH, D, N_MOD * D)) * (0.1 * D ** -0.5),
        "b_ada": nrm(ks[3], (DEPTH, N_MOD * D)) * 0.01,
        "ln_g": 1.0 + 0.01 * nrm(ks[4], (DEPTH, 3, D)),
        "ln_b": 0.01 * nrm(ks[5], (DEPTH, 3, D)),
        "ffn_gu": nrm(ks[6], (DEPTH, 2, D, 2 * D_FF)) * D ** -0.5,
        "ffn_down": nrm(ks[7], (DEPTH, 2, D_FF, D)) * (BETA * D_FF ** -0.5),
        "gmlp_w_in": nrm(ks[8], (N_A, D, GMLP_WIDTH)) * D ** -0.5,
        "gmlp_b_in": 0.01 * nrm(ks[9], (N_A, GMLP_WIDTH)),
        "gmlp_ln_g": 1.0 + 0.01 * nrm(ks[10], (N_A, GMLP_HALF)),
        "gmlp_ln_b": 0.01 * nrm(ks[11], (N_A, GMLP_HALF)),
        "gmlp_w_s": nrm(ks[12], (N_A, GMLP_GROUPS, GMLP_WINDOW, GMLP_WINDOW)) * (0.5 * GMLP_WINDOW ** -0.5),
        "gmlp_b_s": 1.0 + 0.01 * nrm(ks[13], (N_A, GMLP_GROUPS, GMLP_WINDOW)),
        "gmlp_w_out": nrm(ks[14], (N_A, GMLP_HALF, D)) * (BETA * GMLP_HALF ** -0.5),
        "w_ada_kv": nrm(ks[15], (D, 2 * D)) * (0.1 * D ** -0.5),
        "b_ada_kv": 0.01 * nrm(ks[16], (2 * D,)),
        "w_kv": nrm(ks[17], (D, 2 * D)) * D ** -0.5 * v_scale[None, :],
        "attn_w_q": nrm(ks[18], (N_B, D, D)) * D ** -0.5,
        "attn_rel_bias": 0.5 * nrm(ks[19], (N_B, N_HEADS, N_REL)),
        "attn_w_o": nrm(ks[20], (N_B, D, D)) * (BETA * D ** -0.5),
    }


def reference(x, c, w_ada, b_ada, ln_g, ln_b, ffn_gu, ffn_down,
              gmlp_w_in, gmlp_b_in, gmlp_ln_g, gmlp_ln_b, gmlp_w_s, gmlp_b_s, gmlp_w_out,
              w_ada_kv, b_ada_kv, w_kv, attn_w_q, attn_rel_bias, attn_w_o):
    B, S, D = x.shape
    c_act = jax.nn.silu(c)
    k_pad = None
    v_pad = None
    for l in range(DEPTH):
        mod = (c_act @ w_ada[l] + b_ada[l]).reshape(B, 1, N_MOD, D)
        shift = [mod[:, :, 3 * i] for i in range(3)]
        scl = [mod[:, :, 3 * i + 1] for i in range(3)]
        gate = [1.0 + mod[:, :, 3 * i + 2] for i in range(3)]

        h = x * (1.0 + scl[0]) + shift[0]
        y = swiglu(h, ffn_gu[l, 0], ffn_down[l, 0])
        x = layer_norm(ALPHA * x + 0.5 * gate[0] * y, ln_g[l, 0], ln_b[l, 0])

        h = x * (1.0 + scl[1]) + shift[1]
        if l < N_A:
            y = gmlp_mixer(h, gmlp_w_in[l], gmlp_b_in[l], gmlp_ln_g[l], gmlp_ln_b[l],
                           gmlp_w_s[l], gmlp_b_s[l], gmlp_w_out[l])
        else:
            j = l - N_A
            y = chunk_band_attention(h, attn_w_q[j], attn_rel_bias[j], attn_w_o[j], k_pad, v_pad)
        x = layer_norm(ALPHA * x + gate[1] * y, ln_g[l, 1], ln_b[l, 1])

        h = x * (1.0 + scl[2]) + shift[2]
        y = swiglu(h, ffn_gu[l, 1], ffn_down[l, 1])
        x = layer_norm(ALPHA * x + 0.5 * gate[2] * y, ln_g[l, 2], ln_b[l, 2])

        if l == N_A - 1:
            mkv = (c_act @ w_ada_kv + b_ada_kv).reshape(B, 1, 2, D)
            hkv = x * (1.0 + mkv[:, :, 1]) + mkv[:, :, 0]
            kv = hkv @ w_kv
            k = kv[..., :D].reshape(B, S, N_HEADS, HEAD_DIM)
            v = kv[..., D:].reshape(B, S, N_HEADS, HEAD_DIM)
            pad = ((0, 0), (LEFT_PAD, 0), (0, 0), (0, 0))
            k_pad = jnp.pad(k, pad)
            v_pad = jnp.pad(v, pad)
    return x
```

```python
import collections
import contextlib
import numpy as np
import concourse.bass as bass
import concourse.mybir as mybir
from concourse.bass_utils import run_bass_kernel_spmd

F32 = mybir.dt.float32
BF16 = mybir.dt.bfloat16
AF = mybir.ActivationFunctionType
ALU = mybir.AluOpType

D = 1024
S = 2048
NB = 8
DEPTH = 4
N_A = 2
DFF = 2816
NFC = DFF // 128
GW = 4096
GH = 2048
NH = 16
HD = 64
N_REL = 320
ALPHA = (2.0 * DEPTH) ** 0.25
EPS = 1e-5
NDT = D // 128
TT = 512
SLOT = 5632
NSLOT = 3
NEG = -30000.0


class Src:
    __slots__ = ("sem", "n", "name")

    def __init__(self, sem, name):
        self.sem = sem
        self.n = 0
        self.name = name


class Eng(Src):
    __slots__ = ("h", "seen", "selfsync")

    def __init__(self, sem, name, h, selfsync):
        super().__init__(sem, name)
        self.h = h
        self.seen = {}
        self.selfsync = selfsync


class Res:
    __slots__ = ("w", "r")

    def __init__(self, fence=None):
        self.w = dict(fence) if fence else {}
        self.r = {}


class K:
    def __init__(self, nc, es):
        self.nc = nc
        self.es = es
        self.nsem = 0
        mk = lambda nm: es.enter_context(nc.semaphore(nm))
        self.pe = Eng(mk("s_pe"), "pe", nc.tensor, False)
        self.act = Eng(mk("s_act"), "act", nc.scalar, True)
        self.dve = Eng(mk("s_dve"), "dve", nc.vector, True)
        self.pool = Eng(mk("s_pool"), "pool", nc.gpsimd, True)
        self.sp = Eng(mk("s_sp"), "sp", nc.sync, False)
        self.engs = [self.pe, self.act, self.dve, self.pool, self.sp]
        self.dsrcs = []

    def dsrc(self, name):
        s = Src(self.es.enter_context(self.nc.semaphore(name)), name)
        self.dsrcs.append(s)
        return s

    def fence(self):
        f = {e: e.n for e in self.engs if e.n > 0}
        for s in self.dsrcs:
            if s.n > 0:
                f[s] = s.n
        return f

    def _waits(self, eng, reads, writes):
        deps = {}
        for r in reads:
            for s, c in r.w.items():
                if deps.get(s, 0) < c:
                    deps[s] = c
        for w in writes:
            for s, c in w.w.items():
                if deps.get(s, 0) < c:
                    deps[s] = c
            for s, c in w.r.items():
                if deps.get(s, 0) < c:
                    deps[s] = c
        for s, c in deps.items():
            if s is eng and not eng.selfsync:
                continue
            if eng.seen.get(s, 0) < c:
                eng.h.wait_ge(s.sem, c)
                eng.seen[s] = c

    def op(self, eng, fn, reads=(), writes=()):
        self._waits(eng, reads, writes)
        ins = fn()
        eng.n += 1
        ins.then_inc(eng.sem, 1)
        for r in reads:
            r.r[eng] = eng.n
        for w in writes:
            w.w = {eng: eng.n}
            w.r = {}
        return ins

    def dma(self, q, src, out, in_, reads=(), writes=(), **kw):
        self._waits(q, reads, writes)
        ins = q.h.dma_start(out=out, in_=in_, **kw)
        src.n += 16
        ins.then_inc(src.sem, 16)
        for r in reads:
            r.r[src] = src.n
        for w in writes:
            w.w = {src: src.n}
            w.r = {}
        return ins


def build_program(n_layers=DEPTH, dbg=None):
    nc = bass.Bass("TRN2", target_bir_lowering=False)
    es = contextlib.ExitStack()
    k = K(nc, es)

    def dram(name, shape, dt=F32, kind="ExternalInput"):
        return nc.dram_tensor(name, list(shape), dt, kind=kind).ap()

    xT = dram("xT", [D, S])
    pk = dram("pk", [128, PK_COLS])
    w_ada = dram("w_ada", [DEPTH, D, 9 * D])
    ffn_gu = dram("ffn_gu", [DEPTH, 2, D, 2 * DFF])
    ffn_down = dram("ffn_down", [DEPTH, 2, DFF, D])
    gm_w_in = dram("gmlp_w_in", [N_A, D, GW])
    gm_w_out = dram("gmlp_w_out", [N_A, GH, D])
    gm_w_sT = dram("gmlp_w_sT", [N_A, 128, 8 * 128])
    gm_rows = dram("gm_rows", [N_A, 5, GH])
    w_ada_kv = dram("w_ada_kv", [D, 2 * D])
    w_kv = dram("w_kv", [D, 2 * D])
    w_q = dram("attn_w_q", [2, D, D])
    w_o = dram("attn_w_o", [2, D, D])
    biasT = dram("biasT", [2, NH, 5, 128, 128])
    ident = dram("ident", [128, 128])
    outT = dram("outT", [D, S], kind="ExternalOutput")
    import os as _os
    _kvkind = "ExternalOutput" if _os.environ.get("KDBG_KV") else "Internal"
    KTd = dram("KT_scr", [D, S], BF16, kind=_kvkind)
    Vd = dram("V_scr", [S, D], BF16, kind=_kvkind)

    sb = lambda name, shape, dt=F32: es.enter_context(nc.sbuf_tensor(name, list(shape), dt))

    Nt = sb("N", [128, NDT, S])
    Nres = [[Res() for _ in range(S // TT)] for _ in range(NDT)]
    hT = sb("hT", [128, NDT, 2 * TT], BF16)
    hres = [[Res() for _ in range(2)] for _ in range(NDT)]
    pkt = sb("pkt", [128, PK_COLS])
    pk_res = Res()
    wslots = [sb(f"wslot{i}", [128, SLOT], BF16) for i in range(NSLOT)]
    wres = [Res() for _ in range(NSLOT)]
    wsrc = [k.dsrc(f"s_w{i}") for i in range(NSLOT)]
    wstate = {"i": 0}
    NTMP = 1
    sq = [sb(f"sq{i}", [128, TT]) for i in range(2)]
    sq_res = [Res() for _ in range(2)]
    s1 = [sb(f"s1_{i}", [128, TT]) for i in range(2)]
    s2 = [sb(f"s2_{i}", [128, TT]) for i in range(2)]
    st_res = [Res() for _ in range(2)]
    rstd_t = sb("rstd_t", [128, TT])
    stat_res = Res()
    ones32 = sb("ones32", [128, 128])
    onesb = sb("onesb", [128, 128], BF16)
    identb = sb("identb", [128, 128], BF16)
    const_res = Res()
    cact = sb("cact", [128, NDT], BF16)
    cact_res = Res()
    modT = sb("modT", [128, 72])
    mod_res = Res()
    coef = sb("coef", [128, 3 * DEPTH + 1, 5, NDT])
    coef_res = [Res() for _ in range(3 * DEPTH + 1)]
    ctmp = sb("ctmp", [128, NDT])
    ctmp_res = Res()

    psum = [es.enter_context(nc.psum_tensor(f"ps{i}", [128, TT], F32)) for i in range(8)]
    pres = [Res() for _ in range(8)]
    pstate = {"i": 0}

    deferred = collections.deque()
    dstate = {"in": False}

    def bank():
        if not dstate["in"] and deferred and not dstate.get("pause"):
            dstate["in"] = True
            deferred.popleft()()
            dstate["in"] = False
        i = pstate["i"]
        pstate["i"] = (i + 1) % 7
        return psum[i], pres[i]

    def flush_deferred():
        dstate["in"] = True
        while deferred:
            deferred.popleft()()
        dstate["in"] = False

    mod_pb, mod_pr = psum[7], pres[7]
    hooks = collections.deque()
    hstate = {"in": False, "cnt": 0}

    mb_state = {"buf": None, "res": None}
    mb_src = k.dsrc("s_modbuf")

    def run_hook():
        if hstate["in"] or not hooks or mb_state["buf"] is None:
            return
        hstate["in"] = True
        hooks.popleft()()
        hstate["in"] = False

    def flush_hooks():
        hstate["in"] = True
        while hooks:
            hooks.popleft()()
        hstate["in"] = False

    def wload(view_elems, src_aps, rearr=None):
        i = wstate["i"]
        wstate["i"] = (i + 1) % NSLOT
        slot = wslots[i]
        a, b = view_elems
        view = slot[:, 0:a * b].rearrange("p (a b) -> p a b", a=a)
        for (a0, a1, b0, b1, ap) in src_aps:
            k.dma(k.pool, wsrc[i], view[:, a0:a1, b0:b1], ap, writes=[wres[i]], max_dma_last_dim=4096)
        run_hook()
        return view, wres[i]

    def pkcol(name, j=None):
        o, n = PK_OFF[name]
        if j is None:
            return pkt[:, o:o + n]
        return pkt[:, o + j:o + j + 1]

    s_in = k.dsrc("s_in")
    k.dma(k.sp, s_in, pkt[:], pk[:], writes=[pk_res])
    xv = xT.rearrange("(dt p) t -> p dt t", p=128)
    s_x = [k.dsrc(f"s_x{i}") for i in range(4)]
    for tt in range(S // TT):
        k.dma(k.sp, s_x[tt], Nt[:, :, tt * TT:(tt + 1) * TT], xv[:, :, tt * TT:(tt + 1) * TT],
              writes=[Nres[dt][tt] for dt in range(NDT)])
    k.op(k.dve, lambda: nc.vector.memset(ones32[:], 1.0), writes=[const_res])
    k.op(k.dve, lambda: nc.vector.memset(onesb[:], 1.0), writes=[const_res])
    ident_res = Res()
    k.dma(k.pool, k.dsrc("s_ident"), identb[:], ident[:], writes=[ident_res])
    k.op(k.act, lambda: nc.scalar.activation(out=cact[:], in_=pkcol("c"), func=AF.Silu),
         reads=[pk_res], writes=[cact_res])

    gb0 = sb("gb0", [128, 2, NDT])
    gb0_res = Res()
    k.op(k.dve, lambda: nc.vector.memset(gb0[:, 0, :], 1.0), writes=[gb0_res])
    k.op(k.dve, lambda: nc.vector.memset(gb0[:, 1, :], 0.0), writes=[gb0_res])

    MC = 512
    modkv = sb("modkv", [128, 16])
    modkv_res = Res()

    def mod_chunks(wsrc_ap, ncols, bias_ap, dst, dst_res, pcol0, split=None):
        wv = wsrc_ap.rearrange("(kt p) n -> p kt n", p=128)
        out = []

        def chunk(c0):
            if mb_state["buf"] is not None:
                view, vres = mb_state["buf"], mb_state["res"]
                k.dma(k.pool, mb_src, view[:, :, :], wv[:, :, c0:c0 + MC], writes=[vres], max_dma_last_dim=4096)
            else:
                view, vres = wload((NDT, MC), [(0, NDT, 0, MC, wv[:, :, c0:c0 + MC])])
            for jj in range(MC // 128):
                j = c0 // 128 + jj
                for kt in range(NDT):
                    k.op(k.pe, lambda j=j, kt=kt, jj=jj: nc.tensor.matmul(
                        mod_pb[:, pcol0 + j:pcol0 + j + 1], view[:, kt, jj * 128:(jj + 1) * 128], cact[:, kt:kt + 1],
                        start=(kt == 0), stop=(kt == NDT - 1), skip_group_check=True),
                        reads=[vres, cact_res], writes=[mod_pr])

        def final(j0, j1):
            k.op(k.dve, lambda: nc.vector.tensor_tensor(out=dst[:, j0:j1], in0=mod_pb[:, pcol0 + j0:pcol0 + j1],
                                                        in1=bias_ap[:, j0:j1], op=ALU.add),
                 reads=[mod_pr, pk_res], writes=[dst_res])

        nchunk = ncols // MC
        jpc = MC // 128
        for ic in range(nchunk):
            out.append(lambda c0=ic * MC: chunk(c0))
            if split is not None and ic == split - 1:
                out.append(lambda: final(0, split * jpc))
        out.append(lambda: final(0 if split is None else split * jpc, ncols // 128))
        return out

    def make_coef(ci, gprev, bprev, gb_res, ish, isc, igate, wgt, msrc=None, msrc_res=None):
        cr = coef_res[ci]
        c = lambda q: coef[:, ci, q, :]
        msrc = modT if msrc is None else msrc
        mod_res_ = mod_res if msrc_res is None else msrc_res
        md = lambda i: msrc[:, i * NDT:(i + 1) * NDT]
        V = nc.vector
        k.op(k.dve, lambda: V.tensor_scalar(out=ctmp[:], in0=md(isc), scalar1=1.0, scalar2=None, op0=ALU.add),
             reads=[mod_res_], writes=[ctmp_res])
        k.op(k.dve, lambda: V.tensor_tensor(out=c(0), in0=gprev, in1=ctmp[:], op=ALU.mult),
             reads=[ctmp_res, gb_res], writes=[cr])
        k.op(k.dve, lambda: V.tensor_tensor(out=c(1), in0=bprev, in1=ctmp[:], op=ALU.mult),
             reads=[ctmp_res, gb_res], writes=[cr])
        k.op(k.dve, lambda: V.tensor_tensor(out=c(1), in0=c(1), in1=md(ish), op=ALU.add),
             reads=[mod_res_, cr], writes=[cr])
        if igate is not None:
            k.op(k.dve, lambda: V.tensor_scalar(out=c(2), in0=gprev, scalar1=ALPHA, scalar2=None, op0=ALU.mult),
                 reads=[gb_res, cr], writes=[cr])
            k.op(k.dve, lambda: V.tensor_scalar(out=c(3), in0=bprev, scalar1=ALPHA, scalar2=None, op0=ALU.mult),
                 reads=[gb_res, cr], writes=[cr])
            k.op(k.dve, lambda: V.tensor_scalar(out=c(4), in0=md(igate), scalar1=wgt, scalar2=wgt,
                                                op0=ALU.mult, op1=ALU.add),
                 reads=[mod_res_, cr], writes=[cr])

    def modulate(ci, tts, hoff=0):
        for li, gt in enumerate(tts):
            for dt in range(NDT):
                k.op(k.act, lambda dt=dt, li=li, gt=gt: nc.scalar.activation(
                    out=hT[:, dt, (hoff + li) * TT:(hoff + li + 1) * TT], in_=Nt[:, dt, gt * TT:(gt + 1) * TT],
                    func=AF.Identity, scale=coef[:, ci, 0, dt:dt + 1], bias=coef[:, ci, 1, dt:dt + 1]),
                    reads=[Nres[dt][gt], coef_res[ci]], writes=[hres[dt][hoff + li]])

    def prescale(ci, tts):
        for gt in tts:
            for dt in range(NDT):
                k.op(k.act, lambda dt=dt, gt=gt: nc.scalar.activation(
                    out=Nt[:, dt, gt * TT:(gt + 1) * TT], in_=Nt[:, dt, gt * TT:(gt + 1) * TT],
                    func=AF.Identity, scale=coef[:, ci, 2, dt:dt + 1], bias=coef[:, ci, 3, dt:dt + 1]),
                    reads=[coef_res[ci]], writes=[Nres[dt][gt]])

    def epilogue(ci, dt, gt, pb, pr, si):
        nsl = Nt[:, dt, gt * TT:(gt + 1) * TT]
        V = nc.vector
        k.op(k.dve, lambda: V.scalar_tensor_tensor(out=nsl, in0=pb[:], scalar=coef[:, ci, 4, dt:dt + 1], in1=nsl,
                                                   op0=ALU.mult, op1=ALU.add),
             reads=[pr, coef_res[ci]], writes=[Nres[dt][gt]])
        if dt == 0:
            k.op(k.act, lambda: nc.scalar.activation(out=s2[si][:], in_=nsl, func=AF.Square),
                 reads=[Nres[dt][gt]], writes=[st_res[si]])
            k.op(k.dve, lambda: V.tensor_copy(out=s1[si][:], in_=nsl), reads=[Nres[dt][gt]], writes=[st_res[si]])
        else:
            qi = dt % 2
            k.op(k.act, lambda: nc.scalar.activation(out=sq[qi][:], in_=nsl, func=AF.Square),
                 reads=[Nres[dt][gt]], writes=[sq_res[qi]])
            k.op(k.dve, lambda: V.tensor_tensor(out=s1[si][:], in0=s1[si][:], in1=nsl, op=ALU.add),
                 reads=[Nres[dt][gt]], writes=[st_res[si]])
            k.op(k.dve, lambda: V.tensor_tensor(out=s2[si][:], in0=s2[si][:], in1=sq[qi][:], op=ALU.add),
                 reads=[sq_res[qi]], writes=[st_res[si]])

    def ln_finish(gt, si):
        V = nc.vector

        mean_t, nmr_t = s1[si], s2[si]

        def c1():
            p1, r1 = bank()
            p2, r2 = bank()
            k.op(k.pe, lambda: nc.tensor.matmul(p1[:], ones32[:], s1[si][:], start=True, stop=True),
                 reads=[st_res[si], const_res], writes=[r1])
            k.op(k.pe, lambda: nc.tensor.matmul(p2[:], ones32[:], s2[si][:], start=True, stop=True),
                 reads=[st_res[si], const_res], writes=[r2])
            k.op(k.dve, lambda: V.tensor_scalar(out=mean_t[:], in0=p1[:], scalar1=1.0 / D, scalar2=None, op0=ALU.mult),
                 reads=[r1], writes=[st_res[si]])
            k.op(k.act, lambda: nc.scalar.activation(out=nmr_t[:], in_=mean_t[:], func=AF.Square),
                 reads=[st_res[si]], writes=[st_res[si]])
            k.op(k.dve, lambda: V.scalar_tensor_tensor(out=rstd_t[:], in0=p2[:], scalar=1.0 / D, in1=nmr_t[:],
                                                       op0=ALU.mult, op1=ALU.subtract),
                 reads=[r2, st_res[si]], writes=[stat_res])

        def c2():
            pass

        def c3():
            k.op(k.dve, lambda: V.tensor_scalar(out=rstd_t[:], in0=rstd_t[:], scalar1=EPS, scalar2=None, op0=ALU.add),
                 reads=[stat_res], writes=[stat_res])
            k.op(k.act, lambda: nc.scalar.activation(out=rstd_t[:], in_=rstd_t[:], func=AF.Sqrt),
                 reads=[stat_res], writes=[stat_res])

        def c4():
            k.op(k.dve, lambda: V.reciprocal(out=rstd_t[:], in_=rstd_t[:]),
                 reads=[stat_res], writes=[stat_res])
            k.op(k.dve, lambda: V.scalar_tensor_tensor(out=nmr_t[:], in0=mean_t[:], scalar=-1.0, in1=rstd_t[:],
                                                       op0=ALU.mult, op1=ALU.mult),
                 reads=[stat_res, st_res[si]], writes=[st_res[si]])

        def cn(dt):
            nsl = Nt[:, dt, gt * TT:(gt + 1) * TT]
            k.op(k.dve, lambda: V.tensor_tensor(out=nsl, in0=nsl, in1=rstd_t[:], op=ALU.mult),
                 reads=[stat_res], writes=[Nres[dt][gt]])
            k.op(k.dve, lambda: V.tensor_tensor(out=nsl, in0=nsl, in1=nmr_t[:], op=ALU.add),
                 reads=[st_res[si]], writes=[Nres[dt][gt]])

        deferred.extend([c1, c2, c3, c4] + [(lambda dt=dt: cn(dt)) for dt in range(NDT)])

    def ffn(ci, l, which):
        wg = ffn_gu[l, which].rearrange("(kt p) n -> p kt n", p=128)
        wd = ffn_down[l, which].rearrange("(fc p) n -> p fc n", p=128)
        fen = k.fence()
        with nc.sbuf_tensor(f"hid{ci}", [128, NFC, 2 * TT], BF16) as hid, \
                nc.sbuf_tensor(f"sg{ci}", [128, NTMP, TT], F32) as sgt, \
                nc.sbuf_tensor(f"modbuf{ci}", [128, NDT, MC], BF16) as mbt:
            mb_state["buf"], mb_state["res"] = mbt, Res(fen)
            hid_res = [[Res(fen) for _ in range(2)] for _ in range(NFC)]
            sg = [sgt[:, i, :] for i in range(NTMP)]
            sg_res = [Res(fen) for _ in range(NTMP)]
            for half in range(2):
                tts = [2 * half, 2 * half + 1]
                for f0 in range(0, NFC, 2):
                    view, vres = wload((NDT, 512), [
                        (0, NDT, 0, 256, wg[:, :, f0 * 128:f0 * 128 + 256]),
                        (0, NDT, 256, 512, wg[:, :, DFF + f0 * 128:DFF + f0 * 128 + 256])])
                    for fi in range(2):
                        fc = f0 + fi
                        bg = [bank() for _ in range(2)]
                        bu = [bank() for _ in range(2)]
                        for (bks, coff) in ((bg, fi * 128), (bu, 256 + fi * 128)):
                            for kt in range(NDT):
                                for li in range(2):
                                    pb, pr = bks[li]
                                    k.op(k.pe, lambda pb=pb, kt=kt, li=li, coff=coff: nc.tensor.matmul(
                                        pb[:], view[:, kt, coff:coff + 128], hT[:, kt, li * TT:(li + 1) * TT],
                                        start=(kt == 0), stop=(kt == NDT - 1)),
                                        reads=[vres, hres[kt][li]], writes=[pr])
                        for li in range(2):
                            ti = (fc * 2 + li) % NTMP
                            pg, rg = bg[li]
                            pu, ru = bu[li]
                            k.op(k.act, lambda pg=pg, ti=ti: nc.scalar.activation(out=sg[ti], in_=pg[:], func=AF.Silu),
                                 reads=[rg], writes=[sg_res[ti]])
                            k.op(k.dve, lambda pu=pu, ti=ti, fc=fc, li=li: nc.vector.tensor_tensor(
                                out=hid[:, fc, li * TT:(li + 1) * TT], in0=pu[:], in1=sg[ti], op=ALU.mult),
                                reads=[ru, sg_res[ti]], writes=[hid_res[fc][li]])
                yield
                for d0 in range(0, NDT, 2):
                    view, vres = wload((NFC, 256), [(0, NFC, 0, 256, wd[:, :, d0 * 128:d0 * 128 + 256])])
                    for di in range(2):
                        dt = d0 + di
                        for li in range(2):
                            pb, pr = bank()
                            for fc in range(NFC):
                                k.op(k.pe, lambda pb=pb, fc=fc, li=li, di=di: nc.tensor.matmul(
                                    pb[:], view[:, fc, di * 128:(di + 1) * 128], hid[:, fc, li * TT:(li + 1) * TT],
                                    start=(fc == 0), stop=(fc == NFC - 1)),
                                    reads=[vres, hid_res[fc][li]], writes=[pr])
                            epilogue(ci, dt, tts[li], pb, pr, li)
                for li in range(2):
                    ln_finish(tts[li], li)
            flush_hooks()
            mb_state["buf"], mb_state["res"] = None, None

    def gmlp(ci, l):
        V = nc.vector
        w_in = gm_w_in[l].rearrange("(kt p) n -> p kt n", p=128)
        w_out = gm_w_out[l].rearrange("(cc p) n -> p cc n", p=128)
        fen = k.fence()
        with contextlib.ExitStack() as gs:
            gsb = lambda name, shape, dt=F32: gs.enter_context(nc.sbuf_tensor(f"{name}_{l}", list(shape), dt))
            u = gsb("g_u", [128, 16, TT], BF16)
            u_res = [Res(fen) for _ in range(16)]
            vpre = gsb("g_vpre", [128, 4, GH])
            vpre_res = [[Res(fen) for _ in range(4)] for _ in range(4)]
            vb = gsb("g_vb", [128, 2, GH], BF16)
            vb_res = [Res(fen) for _ in range(2)]
            gbc = gsb("g_gbc", [128, GH])
            wsT = gsb("g_wsT", [128, 1024], BF16)
            Lr = gsb("g_L", [128, GH], BF16)
            Rr = gsb("g_R", [128, 1024], BF16)
            smal = gsb("g_small", [128, 4, 4 * 6 + 2 + 2])
            smal_res = [Res(fen) for _ in range(4)]
            gbc_res, ws_res, L_res, R_res, r64_res, rtmp_res = (Res(fen) for _ in range(6))
            sg_ = [k.dsrc(f"s_gm{l}_{i}") for i in range(9)]
            k.dma(k.sp, sg_[0], gbc[:], gm_rows[l, 1:2, :].partition_broadcast(128), writes=[gbc_res])
            k.dma(k.sp, sg_[1], sq[0][32:33, :], gm_rows[l, 3:4, 0:512], writes=[sq_res[0]])
            k.dma(k.sp, sg_[5 + 3 - 1], sq[1][32:33, :], gm_rows[l, 3:4, 512:1024], writes=[sq_res[1]])
            k.dma(k.pool, sg_[2], Lr[0:1, :], gm_rows[l, 2:3, :], writes=[L_res], max_dma_last_dim=4096)
            k.dma(k.pool, sg_[3], Lr[64:65, :], gm_rows[l, 0:1, :], writes=[r64_res], max_dma_last_dim=4096)
            k.dma(k.pool, sg_[4], wsT[:], gm_w_sT[l], writes=[ws_res], max_dma_last_dim=4096)
            L12_res = [Res(fen), Res(fen)]
            sg_ones = [k.dsrc(f"s_gm{l}_ones{i}") for i in range(2)]
            for rr in (1, 2):
                k.dma(k.pool, sg_ones[rr - 1], Lr[rr:rr + 1, :], gm_rows[l, 4:5, :], writes=[L12_res[rr - 1]],
                      max_dma_last_dim=4096)
            wsT3 = wsT[:].rearrange("p (g t) -> p g t", g=8)
            k.op(k.dve, lambda: V.memset(wsT3[64:128, :, 0:64], 0.0), writes=[ws_res])
            for hh in range(2):
                pb, pr = bank()
                k.op(k.pe, lambda pb=pb, hh=hh: nc.tensor.matmul(pb[0:1, :], onesb[:, 0:1], wsT[:, hh * 512:(hh + 1) * 512],
                                                                start=True, stop=True),
                     reads=[ws_res, const_res], writes=[pr])
                k.op(k.dve, lambda pb=pb, hh=hh: V.tensor_copy(out=Rr[0:1, hh * 512:(hh + 1) * 512], in_=pb[0:1, :]),
                     reads=[pr], writes=[R_res])
            t32 = Rr[32:33, :]
            for hh, src_t in ((0, sq[0]), (1, sq[1])):
                k.op(k.dve, lambda hh=hh, src_t=src_t: V.tensor_copy(out=Rr[32:33, hh * 512:(hh + 1) * 512], in_=src_t[32:33, :]),
                     reads=[sq_res[0], sq_res[1]], writes=[rtmp_res])
            k.dma(k.sp, sg_[6], Rr[1:2, :], t32, reads=[rtmp_res], writes=[R_res])
            for hh, src_t in ((0, sq[0]), (1, sq[1])):
                k.op(k.dve, lambda hh=hh, src_t=src_t: V.tensor_tensor(
                    out=Rr[32:33, hh * 512:(hh + 1) * 512], in0=src_t[32:33, :], in1=Rr[32:33, hh * 512:(hh + 1) * 512],
                    op=ALU.subtract),
                    reads=[sq_res[0], sq_res[1]], writes=[rtmp_res])
            k.dma(k.sp, sg_[8], Rr[2:3, :], t32, reads=[rtmp_res], writes=[R_res])

            for p in range(S // TT):
                gt = p
                ho = p % 2
                si = p % 2
                hsl = lambda kt, a, b, ho=ho: hT[:, kt, ho * TT + a:ho * TT + b]

                def uphase():
                    for c0 in range(0, 16, 4):
                        view, vres = wload((NDT, 512), [(0, NDT, 0, 512, w_in[:, :, c0 * 128:c0 * 128 + 512])])
                        for cj in range(4):
                            cc = c0 + cj
                            pb, pr = bank()
                            for kt in range(NDT):
                                k.op(k.pe, lambda pb=pb, kt=kt, cj=cj: nc.tensor.matmul(
                                    pb[:], view[:, kt, cj * 128:(cj + 1) * 128], hsl(kt, 0, TT),
                                    start=(kt == 0), stop=(kt == NDT - 1)),
                                    reads=[vres, hres[kt][ho]], writes=[pr])
                            o, _ = PK_OFF["b_in_u"]
                            bcol = pkt[:, o + l * 16 + cc:o + l * 16 + cc + 1]
                            k.op(k.act, lambda pb=pb, cc=cc, bcol=bcol: nc.scalar.activation(
                                out=u[:, cc, :], in_=pb[:], func=AF.Gelu, bias=bcol),
                                reads=[pr, pk_res], writes=[u_res[cc]])

                def vphase():
                    for vc in range(4):
                        view, vres = wload((NDT, 512), [(0, NDT, 0, 512, w_in[:, :, GH + vc * 512:GH + (vc + 1) * 512])])
                        for w in range(4):
                            pb, pr = bank()
                            for kt in range(NDT):
                                k.op(k.pe, lambda pb=pb, kt=kt, w=w: nc.tensor.matmul(
                                    pb[:], hsl(kt, w * 128, (w + 1) * 128), view[:, kt, :],
                                    start=(kt == 0), stop=False),
                                    reads=[vres, hres[kt][ho]], writes=[pr])
                            k.op(k.pe, lambda pb=pb, vc=vc: nc.tensor.matmul(
                                pb[:], onesb[64:65, :], Lr[64:65, vc * 512:(vc + 1) * 512], start=False, stop=True),
                                reads=[r64_res, const_res], writes=[pr])
                            k.op(k.act, lambda pb=pb, w=w, vc=vc: nc.scalar.activation(
                                out=vpre[:, w, vc * 512:(vc + 1) * 512], in_=pb[:], func=AF.Gelu),
                                reads=[pr], writes=[vpre_res[w][vc]])

                def stats_all():
                    for w in range(4):
                        sm = smal[:, w, :]
                        for vc in range(4):
                            k.op(k.dve, lambda vc=vc, w=w, sm=sm: V.bn_stats(
                                out=sm[:, vc * 6:(vc + 1) * 6], in_=vpre[:, w, vc * 512:(vc + 1) * 512]),
                                reads=[vpre_res[w][vc]], writes=[smal_res[0]])
                        k.op(k.dve, lambda sm=sm: V.bn_aggr(out=sm[:, 24:26], in_=sm[:, 0:24]),
                             reads=[smal_res[0]], writes=[smal_res[0]])
                    var4, rs4, mu4, nb4 = smal[:, :, 25], smal[:, :, 26], smal[:, :, 24], smal[:, :, 27]
                    k.op(k.dve, lambda: V.tensor_scalar(out=rs4, in0=var4, scalar1=EPS, scalar2=None, op0=ALU.add),
                         reads=[smal_res[0]], writes=[smal_res[0]])
                    k.op(k.act, lambda: nc.scalar.activation(out=rs4, in_=rs4, func=AF.Sqrt),
                         reads=[smal_res[0]], writes=[smal_res[0]])
                    k.op(k.dve, lambda: V.reciprocal(out=rs4, in_=rs4), reads=[smal_res[0]], writes=[smal_res[0]])
                    k.op(k.dve, lambda: V.scalar_tensor_tensor(out=nb4, in0=mu4, scalar=-1.0, in1=rs4,
                                                               op0=ALU.mult, op1=ALU.mult),
                         reads=[smal_res[0]], writes=[smal_res[0]])

                def norm(w):
                    sm = smal[:, w, :]
                    k.op(k.act, lambda: nc.scalar.activation(
                        out=vpre[:, w, :], in_=vpre[:, w, :], func=AF.Identity, scale=sm[:, 26:27], bias=sm[:, 27:28]),
                        reads=[smal_res[0]], writes=vpre_res[w])

                def vbmul(w):
                    wi = w % 2
                    k.op(k.dve, lambda: V.tensor_tensor(out=vb[:, wi, :], in0=vpre[:, w, :], in1=gbc[:], op=ALU.mult),
                         reads=vpre_res[w] + [gbc_res], writes=[vb_res[wi]])

                def spatial(wi, w):
                    for cg in range(4):
                        pb, pr = bank()
                        for cj in range(4):
                            cc = cg * 4 + cj
                            g = cc // 2
                            k.op(k.pe, lambda pb=pb, cj=cj, cc=cc, g=g: nc.tensor.matmul(
                                pb[:, cj * 128:(cj + 1) * 128], vb[:, wi, cc * 128:(cc + 1) * 128],
                                wsT[:, g * 128:(g + 1) * 128], start=(cj == 0), stop=False, skip_group_check=True),
                                reads=[vb_res[wi], ws_res], writes=[pr])
                        for cj in range(4):
                            cc = cg * 4 + cj
                            g = cc // 2
                            k.op(k.pe, lambda pb=pb, cj=cj, cc=cc, g=g: nc.tensor.matmul(
                                pb[:, cj * 128:(cj + 1) * 128], Lr[0:3, cc * 128:(cc + 1) * 128],
                                Rr[0:3, g * 128:(g + 1) * 128], start=False, stop=True, skip_group_check=True),
                                reads=[L_res, R_res] + L12_res, writes=[pr])
                        usl = u[:, cg * 4:(cg + 1) * 4, w * 128:(w + 1) * 128]
                        k.op(k.dve, lambda pb=pb, usl=usl: V.tensor_tensor(
                            out=usl, in0=pb[:].rearrange("p (a b) -> p a b", a=4), in1=usl, op=ALU.mult),
                            reads=[pr], writes=u_res[cg * 4:(cg + 1) * 4])

                dstate["pause"] = True
                vphase()
                stats_all()
                norm(0)
                vbmul(0)
                norm(1)
                vbmul(1)
                norm(2)
                norm(3)
                dstate["pause"] = False
                uphase()
                spatial(0, 0)
                vbmul(2)
                spatial(1, 1)
                vbmul(3)
                spatial(0, 2)
                spatial(1, 3)
                yield
                for d0 in range(0, NDT, 2):
                    view, vres = wload((16, 256), [(0, 16, 0, 256, w_out[:, :, d0 * 128:d0 * 128 + 256])])
                    for di in range(2):
                        dt = d0 + di
                        pb, pr = bank()
                        for cc in range(16):
                            k.op(k.pe, lambda pb=pb, cc=cc, di=di: nc.tensor.matmul(
                                pb[:], view[:, cc, di * 128:(di + 1) * 128], u[:, cc, :],
                                start=(cc == 0), stop=(cc == 15)),
                                reads=[vres, u_res[cc]], writes=[pr])
                        epilogue(ci, dt, gt, pb, pr, si)
                ln_finish(gt, si)

    kv_res = Res()
    s_kvst = [k.dsrc(f"s_kvst{i}") for i in range(4)]

    def kvload(view_elems, out_in_pairs):
        i = wstate["i"]
        wstate["i"] = (i + 1) % NSLOT
        a, b = view_elems
        view = wslots[i][:, 0:a * b].rearrange("p (a b) -> p a b", a=a)
        for (sl, ap) in out_in_pairs:
            k.dma(k.pool, wsrc[i], sl(view), ap, reads=[kv_res], writes=[wres[i]])
        return view, wres[i]

    def kv_produce(ci):
        KTv = KTd.rearrange("(dt p) s -> p dt s", p=128)
        wkv = w_kv.rearrange("(kt p) n -> p kt n", p=128)
        fen = k.fence()
        with nc.sbuf_tensor("kvst", [128, 4, TT], BF16) as kvst:
            st_r = [Res(fen) for _ in range(4)]
            cnt = [0]

            def evac_store(pb, pr, dst):
                i = cnt[0] % 4
                cnt[0] += 1
                if i % 2 == 0:
                    k.op(k.act, lambda: nc.scalar.copy(out=kvst[:, i, :], in_=pb[:]), reads=[pr], writes=[st_r[i]])
                else:
                    k.op(k.dve, lambda: nc.vector.tensor_copy(out=kvst[:, i, :], in_=pb[:]), reads=[pr], writes=[st_r[i]])
                k.dma(k.sp, s_kvst[i], dst, kvst[:, i, :], reads=[st_r[i]])

            for half in range(2):
                tts = [2 * half, 2 * half + 1]
                for c0 in (0, 512):
                    view, vres = wload((NDT, 512), [(0, NDT, 0, 512, wkv[:, :, c0:c0 + 512])])
                    for dj in range(4):
                        dt = c0 // 128 + dj
                        for li in range(2):
                            pb, pr = bank()
                            for kt in range(NDT):
                                k.op(k.pe, lambda pb=pb, kt=kt, dj=dj, li=li: nc.tensor.matmul(
                                    pb[:], view[:, kt, dj * 128:(dj + 1) * 128], hT[:, kt, li * TT:(li + 1) * TT],
                                    start=(kt == 0), stop=(kt == NDT - 1)),
                                    reads=[vres, hres[kt][li]], writes=[pr])
                            evac_store(pb, pr, KTv[:, dt, tts[li] * TT:(tts[li] + 1) * TT])
                for c0 in (0, 512):
                    view, vres = wload((NDT, 512), [(0, NDT, 0, 512, wkv[:, :, D + c0:D + c0 + 512])])
                    for tb in range(8):
                        li = tb // 4
                        tok0 = half * 1024 + tb * 128
                        pb, pr = bank()
                        for kt in range(NDT):
                            k.op(k.pe, lambda pb=pb, kt=kt, tb=tb: nc.tensor.matmul(
                                pb[:], hT[:, kt, tb * 128:(tb + 1) * 128], view[:, kt, :],
                                start=(kt == 0), stop=(kt == NDT - 1)),
                                reads=[vres, hres[kt][li]], writes=[pr])
                        evac_store(pb, pr, Vd[tok0:tok0 + 128, c0:c0 + 512])
                if half == 1:
                    kv_res.w = {sx: sx.n for sx in s_kvst}
                yield

    bias_state = {"buf": None, "res": Res()}

    def bias_hooks(l):
        j_l = l - N_A
        if bias_state["buf"] is None:
            bias_state["buf"] = sb("a_bias", [128, NH * 5, 128], BF16)
            bias_state["res"] = Res(k.fence())
        biasb, bias_res = bias_state["buf"], bias_state["res"]
        bsrc = biasT[j_l].rearrange("h j r t -> r (h j) t")
        s_b = [k.dsrc(f"s_bias{l}_{i}") for i in range(8)]
        piece_res = [Res() for _ in range(8)]
        b4 = biasb[:].rearrange("p (h j) t -> p h j t", j=5)
        out = []

        def piece(i):
            h0 = 2 * i
            k.dma(k.pool, s_b[i], biasb[:, h0 * 5:(h0 + 2) * 5, :], bsrc[:, h0 * 5:(h0 + 2) * 5, :],
                  reads=[], writes=[piece_res[i], bias_res] if i == 0 else [piece_res[i]])

        def masks():
            k.op(k.dve, lambda: nc.vector.memset(b4[64:128, :, 4, 0:64], NEG), reads=piece_res, writes=[bias_res])
            k.op(k.dve, lambda: nc.vector.memset(b4[0:64, :, 0, 64:128], NEG), writes=[bias_res])

        for i in range(8):
            out.append(lambda i=i: piece(i))
        out.append(masks)
        return out

    def attention(ci, l):
        V = nc.vector
        j_l = l - N_A
        wq = w_q[j_l].rearrange("(kt p) n -> p kt n", p=128)
        wo = w_o[j_l].rearrange("(kt p) n -> p kt n", p=128)
        KTv = KTd.rearrange("(dt p) s -> p dt s", p=128)
        Vv = Vd.rearrange("(kt p) c -> p kt c", p=128)
        fen = k.fence()
        with contextlib.ExitStack() as gs:
            gsb = lambda name, shape, dt=F32: gs.enter_context(nc.sbuf_tensor(f"{name}_{l}", list(shape), dt))
            qT = gsb("a_qT", [128, NDT, TT], BF16)
            q_res = [Res(fen) for _ in range(NDT)]
            oT = gsb("a_oT", [128, NDT, TT], BF16)
            o_res = [Res(fen) for _ in range(NDT)]
            biasb, bias_res = bias_state["buf"], bias_state["res"]
            NPT = 2
            PT = [gsb(f"a_PT{i}", [128, 5, 1024], BF16) for i in range(NPT)]
            pt_res = [[Res(fen) for _ in range(5)] for _ in range(NPT)]
            rinv = [gsb(f"a_rinv{i}", [128, TT]) for i in range(2)]
            rinv_res = [Res(fen) for _ in range(2)]
            for qp in range(S // TT):
                gt = qp
                ho = qp % 2
                si = qp % 2
                for c0 in (0, 512):
                    view, vres = wload((NDT, 512), [(0, NDT, 0, 512, wq[:, :, c0:c0 + 512])])
                    for dj in range(4):
                        dt = c0 // 128 + dj
                        pb, pr = bank()
                        for kt in range(NDT):
                            k.op(k.pe, lambda pb=pb, kt=kt, dj=dj: nc.tensor.matmul(
                                pb[:], view[:, kt, dj * 128:(dj + 1) * 128], hT[:, kt, ho * TT:(ho + 1) * TT],
                                start=(kt == 0), stop=(kt == NDT - 1)),
                                reads=[vres, hres[kt][ho]], writes=[pr])
                        k.op(k.act, lambda pb=pb, dt=dt: nc.scalar.activation(
                            out=qT[:, dt, :], in_=pb[:], func=AF.Copy, scale=HD ** -0.5),
                            reads=[pr], writes=[q_res[dt]])
                band = {}

                def qk(ku):
                    m, hg = ku // 2, ku % 2
                    pm = 4 * qp + m
                    nj = min(5, pm + 1)
                    j0 = 5 - nj
                    kt0 = pm - 4 + j0
                    if hg == 0:
                        Kb, kres = kvload((NDT, nj * 128), [(lambda v: v[:, :, :], KTv[:, :, kt0 * 128:(pm + 1) * 128])])
                        Vb, vres_ = kvload((nj, D), [(lambda v: v[:, :, :], Vv[:, kt0:pm + 1, :])])
                        band[m] = (Kb, kres, Vb, vres_)
                    Kb, kres, Vb, vres_ = band[m]
                    qs = slice(m * 128, (m + 1) * 128)
                    pi = ku % NPT
                    P = PT[pi]
                    for jj in range(nj):
                        j = j0 + jj
                        bA = bank()
                        bB = bank()
                        for di in range(4):
                            dt = hg * 4 + di
                            for hh, (pb, pr) in ((0, bA), (1, bB)):
                                k.op(k.pe, lambda pb=pb, dt=dt, hh=hh, di=di, jj=jj: nc.tensor.matmul(
                                    pb[:, di * 128:(di + 1) * 128],
                                    Kb[hh * 64:(hh + 1) * 64, dt, jj * 128:(jj + 1) * 128],
                                    qT[hh * 64:(hh + 1) * 64, dt, qs], start=(di == 0), stop=False,
                                    skip_group_check=True),
                                    reads=[kres, q_res[dt]], writes=[pr])
                        for hh, (pb, pr) in ((0, bA), (1, bB)):
                            for di in range(4):
                                h = (hg * 4 + di) * 2 + hh
                                k.op(k.pe, lambda pb=pb, di=di, h=h, j=j: nc.tensor.matmul(
                                    pb[:, di * 128:(di + 1) * 128], identb[:], biasb[:, h * 5 + j, :],
                                    start=False, stop=True, skip_group_check=True),
                                    reads=[bias_res, ident_res], writes=[pr])
                            k.op(k.act, lambda pb=pb, hh=hh, jj=jj, P=P: nc.scalar.activation(
                                out=P[:, jj, hh * 512:(hh + 1) * 512], in_=pb[:], func=AF.Exp),
                                reads=[pr], writes=[pt_res[pi][jj]])

                def spv(ku):
                    m, hg = ku // 2, ku % 2
                    pm = 4 * qp + m
                    nj = min(5, pm + 1)
                    Kb, kres, Vb, vres_ = band[m]
                    qs = slice(m * 128, (m + 1) * 128)
                    pi = ku % NPT
                    P = PT[pi]
                    sbk, sres = bank()
                    obk, ores = bank()
                    for jj in range(nj):
                        for hh in range(2):
                            k.op(k.pe, lambda hh=hh, jj=jj: nc.tensor.matmul(
                                sbk[hh * 64:(hh + 1) * 64, :], onesb[:, 0:64], P[:, jj, hh * 512:(hh + 1) * 512],
                                start=(jj == 0), stop=(jj == nj - 1), skip_group_check=True),
                                reads=[pt_res[pi][jj], const_res], writes=[sres])
                    for jj in range(nj):
                        for di in range(4):
                            dt = hg * 4 + di
                            for hh in range(2):
                                k.op(k.pe, lambda dt=dt, hh=hh, di=di, jj=jj: nc.tensor.matmul(
                                    obk[hh * 64:(hh + 1) * 64, di * 128:(di + 1) * 128],
                                    Vb[:, jj, dt * 128 + hh * 64:dt * 128 + (hh + 1) * 64],
                                    P[:, jj, hh * 512 + di * 128:hh * 512 + (di + 1) * 128],
                                    start=(jj == 0 and di == 0), stop=(jj == nj - 1), skip_group_check=True),
                                    reads=[vres_, pt_res[pi][jj]], writes=[ores])
                    ri = ku % 2
                    k.op(k.dve, lambda: V.reciprocal(out=rinv[ri][:], in_=sbk[:]),
                         reads=[sres], writes=[rinv_res[ri]])
                    k.op(k.dve, lambda: V.tensor_tensor(
                        out=oT[:, hg * 4:(hg + 1) * 4, qs],
                        in0=obk[:].rearrange("p (a b) -> p a b", a=4),
                        in1=rinv[ri][:].rearrange("p (a b) -> p a b", a=4), op=ALU.mult),
                        reads=[ores, rinv_res[ri]], writes=o_res[hg * 4:(hg + 1) * 4])

                qk(0)
                for ku in range(8):
                    if ku + 1 < 8:
                        qk(ku + 1)
                    spv(ku)
                    if ku < 7:
                        yield 2
                yield
                for c0 in (0, 512):
                    view, vres = wload((NDT, 512), [(0, NDT, 0, 512, wo[:, :, c0:c0 + 512])])
                    for dj in range(4):
                        dt = c0 // 128 + dj
                        pb, pr = bank()
                        for kt in range(NDT):
                            k.op(k.pe, lambda pb=pb, kt=kt, dj=dj: nc.tensor.matmul(
                                pb[:], view[:, kt, dj * 128:(dj + 1) * 128], oT[:, kt, :],
                                start=(kt == 0), stop=(kt == NDT - 1)),
                                reads=[vres, o_res[kt]], writes=[pr])
                        epilogue(ci, dt, gt, pb, pr, si)
                ln_finish(gt, si)

    def pro(ci, tts, hoff=0, scale=True):
        pieces = []
        for li, gt in enumerate(tts):
            for dt in range(NDT):
                pieces.append(lambda dt=dt, li=li, gt=gt: k.op(k.act, lambda: nc.scalar.activation(
                    out=hT[:, dt, (hoff + li) * TT:(hoff + li + 1) * TT], in_=Nt[:, dt, gt * TT:(gt + 1) * TT],
                    func=AF.Identity, scale=coef[:, ci, 0, dt:dt + 1], bias=coef[:, ci, 1, dt:dt + 1]),
                    reads=[Nres[dt][gt], coef_res[ci]], writes=[hres[dt][hoff + li]]))
        if scale:
            for gt in tts:
                for dt in range(NDT):
                    pieces.append(lambda dt=dt, gt=gt: k.op(k.act, lambda: nc.scalar.activation(
                        out=Nt[:, dt, gt * TT:(gt + 1) * TT], in_=Nt[:, dt, gt * TT:(gt + 1) * TT],
                        func=AF.Identity, scale=coef[:, ci, 2, dt:dt + 1], bias=coef[:, ci, 3, dt:dt + 1]),
                        reads=[coef_res[ci]], writes=[Nres[dt][gt]]))
        return pieces

    def ln_affine(l, i):
        o, _ = PK_OFF["ln"]
        base = o + ((l * 3 + i) * 2) * NDT
        return pkt[:, base:base + NDT], pkt[:, base + NDT:base + 2 * NDT], pk_res

    def prev_affine(l, i):
        if l == 0 and i == 0:
            return gb0[:, 0, :], gb0[:, 1, :], gb0_res
        return ln_affine(l, i - 1) if i > 0 else ln_affine(l - 1, 2)

    def coefs(l, ii):
        for i in ii:
            g_, b_, r_ = prev_affine(l, i)
            make_coef(3 * l + i, g_, b_, r_, 3 * i, 3 * i + 1, 3 * i + 2, 0.5 if i != 1 else 1.0)

    def layer_mod_hooks(l, split=None):
        return mod_chunks(w_ada[l], 9 * D, pkcol("b_ada")[:, l * 72:(l + 1) * 72], modT, mod_res, 0, split=split)

    def kv_mod_hooks():
        return mod_chunks(w_ada_kv, 2 * D, pkcol("b_ada_kv"), modkv, modkv_res, 72)

    plan = []
    for l in range(n_layers):
        for i in range(3):
            ci = 3 * l + i
            pre = None
            post = None
            if i == 0 and l > 0:
                pre = (lambda l=l: (flush_hooks(), coefs(l, (0, 1, 2))))
            if l == 0 and i == 1:
                pre = (lambda: (flush_hooks(), coefs(0, (1, 2))))
            if i != 1:
                pros = [pro(ci, [0, 1]), pro(ci, [2, 3])]
                gen = (lambda ci=ci, l=l, i=i: ffn(ci, l, 0 if i == 0 else 1))
            elif l < N_A:
                pros = [pro(ci, [p], hoff=p % 2) for p in range(4)]
                gen = (lambda ci=ci, l=l: gmlp(ci, l))
            else:
                pros = [pro(ci, [p], hoff=p % 2) for p in range(4)]
                gen = (lambda ci=ci, l=l: attention(ci, l))
            start = None
            if i == 0 and l >= N_A:
                def start(l=l):
                    hooks.extend(bias_hooks(l))
            if i == 1 and l + 1 < n_layers:
                def start(l=l):
                    if l == N_A - 1:
                        hooks.extend(kv_mod_hooks())
                    hooks.extend(layer_mod_hooks(l + 1))
            plan.append([pre, pros, gen, start])
        if l == N_A - 1 and n_layers > N_A:
            def kvpre(l=l):
                flush_hooks()
                g_, b_, r_ = ln_affine(l, 2)
                make_coef(3 * DEPTH, g_, b_, r_, 0, 1, None, None, msrc=modkv, msrc_res=modkv_res)
            plan.append([kvpre, [pro(3 * DEPTH, [0, 1], scale=False), pro(3 * DEPTH, [2, 3], scale=False)],
                         (lambda: kv_produce(3 * DEPTH)), None])

    m0 = layer_mod_hooks(0, split=6)
    for f in m0[:7]:
        f()
    hooks.extend(m0[7:])
    coefs(0, (0,))

    flat = []
    for ui, (pre, pros, gen, start) in enumerate(plan):
        for pi_, pf in enumerate(pros):
            flat.append([pre if pi_ == 0 else None, list(pf), False])
    fstate = {"i": 0}

    def emit_next_prologue(nparts=None):
        fi = fstate["i"]
        if fi >= len(flat):
            return
        ent = flat[fi]
        if not ent[2]:
            flush_deferred()
            if ent[0] is not None:
                ent[0]()
            ent[2] = True
        n = len(ent[1]) if nparts is None else min(nparts, len(ent[1]))
        for _ in range(n):
            ent[1].pop(0)()
        if nparts is None:
            fstate["i"] = fi + 1

    emit_next_prologue()
    for (pre, pros, gen, start) in plan:
        if start is not None:
            start()
        for y in gen():
            emit_next_prologue(y)
    flush_hooks()
    flush_deferred()
    gprev, bprev, gb_res = ln_affine(n_layers - 1, 2)

    ov = outT.rearrange("(dt p) t -> p dt t", p=128)
    s_out = k.dsrc("s_out")
    ctmp2 = sb("ctmp2", [128, 2, NDT])
    c2_res = Res()
    k.op(k.dve, lambda: nc.vector.tensor_copy(out=ctmp2[:, 0, :], in_=gprev), reads=[gb_res], writes=[c2_res])
    k.op(k.dve, lambda: nc.vector.tensor_copy(out=ctmp2[:, 1, :], in_=bprev), reads=[gb_res, c2_res], writes=[c2_res])
    for gt in range(S // TT):
        for dt in range(NDT):
            nsl = Nt[:, dt, gt * TT:(gt + 1) * TT]
            k.op(k.act, lambda nsl=nsl, dt=dt: nc.scalar.activation(
                out=nsl, in_=nsl, func=AF.Identity, scale=ctmp2[:, 0, dt:dt + 1], bias=ctmp2[:, 1, dt:dt + 1]),
                reads=[c2_res], writes=[Nres[dt][gt]])
            k.dma(k.sp, s_out, ov[:, dt, gt * TT:(gt + 1) * TT], nsl, reads=[Nres[dt][gt]])
    k.sp.h.wait_ge(s_out.sem, s_out.n)
    es.close()
    return nc


PK_OFF = {}
_o = 0
for _name, _n in (("c", NDT), ("b_ada", DEPTH * 72), ("ln", DEPTH * 3 * 2 * NDT), ("b_in_u", N_A * 16),
                  ("b_ada_kv", 16)):
    PK_OFF[_name] = (_o, _n)
    _o += _n
PK_COLS = _o


def _fm(v):
    v = np.asarray(v, np.float32)
    lead = v.shape[:-1]
    n = v.shape[-1] // 128
    a = v.reshape(lead + (n, 128))
    a = np.moveaxis(a, -1, 0)
    return np.ascontiguousarray(a.reshape(128, -1))


def make_in_maps(inputs):
    f = lambda a: np.ascontiguousarray(np.asarray(a, np.float32))
    x = f(inputs["x"])
    c = f(inputs["c"])
    b_ada = f(inputs["b_ada"])
    ln = np.stack([f(inputs["ln_g"]), f(inputs["ln_b"])], axis=2)
    b_in = f(inputs["gmlp_b_in"])
    rel = f(inputs["attn_rel_bias"])
    j = np.arange(5)[:, None, None]
    r = np.arange(128)[None, :, None]
    t = np.arange(128)[None, None, :]
    dist = 128 * (4 - j) + t - r
    idx = np.clip(dist, -(HD - 1), 4 * 64) + (HD - 1)
    biasT = np.ascontiguousarray(rel[:, :, idx])
    gm_rows = np.zeros((N_A, 5, GH), np.float32)
    gm_rows[:, 4] = 1.0
    gm_rows[:, 0] = b_in[:, GH:]
    gm_rows[:, 1] = f(inputs["gmlp_ln_g"])
    gm_rows[:, 2] = f(inputs["gmlp_ln_b"])
    gm_rows[:, 3, :1024] = f(inputs["gmlp_b_s"]).reshape(N_A, 1024)
    shared = {
        "w_ada": f(inputs["w_ada"]), "ffn_gu": f(inputs["ffn_gu"]), "ffn_down": f(inputs["ffn_down"]),
        "gmlp_w_in": f(inputs["gmlp_w_in"]), "gmlp_w_out": f(inputs["gmlp_w_out"]),
        "gmlp_w_sT": np.ascontiguousarray(f(inputs["gmlp_w_s"]).transpose(0, 3, 1, 2).reshape(N_A, 128, 1024)),
        "gm_rows": gm_rows,
        "w_ada_kv": f(inputs["w_ada_kv"]), "w_kv": f(inputs["w_kv"]),
        "attn_w_q": f(inputs["attn_w_q"]), "attn_w_o": f(inputs["attn_w_o"]), "biasT": biasT,
        "ident": np.eye(128, dtype=np.float32),
    }
    maps = []
    for b in range(NB):
        pkb = np.concatenate([
            _fm(c[b]), _fm(b_ada), _fm(ln), _fm(b_in[:, :GH]), _fm(f(inputs["b_ada_kv"]))], axis=1)
        assert pkb.shape == (128, PK_COLS), pkb.shape
        m = dict(shared)
        m["xT"] = np.ascontiguousarray(x[b].T)
        m["pk"] = np.ascontiguousarray(pkb)
        maps.append(m)
    return maps


_NC_CACHE = {}


def run(inputs, n_layers=DEPTH, dbg=None, trace=False):
    key = (n_layers, dbg)
    if key not in _NC_CACHE:
        _NC_CACHE[key] = build_program(n_layers, dbg)
    nc = _NC_CACHE[key]
    maps = make_in_maps(inputs)
    res = run_bass_kernel_spmd(nc, maps, core_ids=list(range(NB)), trace=trace)
    out = np.stack([np.ascontiguousarray(r["outT"].T) for r in res.results], axis=0)
    return out.astype(np.float32), res


def kernel(**inputs):
    out, _ = run(inputs)
    return out
```

```python
import collections
import contextlib
import numpy as np
import concourse.bass as bass
import concourse.mybir as mybir
from concourse.bass_utils import run_bass_kernel_spmd

F32 = mybir.dt.float32
BF16 = mybir.dt.bfloat16
AF = mybir.ActivationFunctionType
ALU = mybir.AluOpType

D = 1024
S = 2048
NB = 8
DEPTH = 4
N_A = 2
DFF = 2816
NFC = DFF // 128
GW = 4096
GH = 2048
NH = 16
HD = 64
N_REL = 320
ALPHA = (2.0 * DEPTH) ** 0.25
EPS = 1e-5
NDT = D // 128
TT = 512
SLOT = 5632
NSLOT = 3
NEG = -30000.0


class Src:
    __slots__ = ("sem", "n", "name")

    def __init__(self, sem, name):
        self.sem = sem
        self.n = 0
        self.name = name


class Eng(Src):
    __slots__ = ("h", "seen", "selfsync")

    def __init__(self, sem, name, h, selfsync):
        super().__init__(sem, name)
        self.h = h
        self.seen = {}
        self.selfsync = selfsync


class Res:
    __slots__ = ("w", "r")

    def __init__(self, fence=None):
        self.w = dict(fence) if fence else {}
        self.r = {}


class K:
    def __init__(self, nc, es):
        self.nc = nc
        self.es = es
        self.nsem = 0
        mk = lambda nm: es.enter_context(nc.semaphore(nm))
        self.pe = Eng(mk("s_pe"), "pe", nc.tensor, False)
        self.act = Eng(mk("s_act"), "act", nc.scalar, True)
        self.dve = Eng(mk("s_dve"), "dve", nc.vector, True)
        self.pool = Eng(mk("s_pool"), "pool", nc.gpsimd, True)
        self.sp = Eng(mk("s_sp"), "sp", nc.sync, False)
        self.engs = [self.pe, self.act, self.dve, self.pool, self.sp]
        self.dsrcs = []

    def dsrc(self, name):
        s = Src(self.es.enter_context(self.nc.semaphore(name)), name)
        self.dsrcs.append(s)
        return s

    def fence(self):
        f = {e: e.n for e in self.engs if e.n > 0}
        for s in self.dsrcs:
            if s.n > 0:
                f[s] = s.n
        return f

    def _waits(self, eng, reads, writes):
        deps = {}
        for r in reads:
            for s, c in r.w.items():
                if deps.get(s, 0) < c:
                    deps[s] = c
        for w in writes:
            for s, c in w.w.items():
                if deps.get(s, 0) < c:
                    deps[s] = c
            for s, c in w.r.items():
                if deps.get(s, 0) < c:
                    deps[s] = c
        for s, c in deps.items():
            if s is eng and not eng.selfsync:
                continue
            if eng.seen.get(s, 0) < c:
                eng.h.wait_ge(s.sem, c)
                eng.seen[s] = c

    def op(self, eng, fn, reads=(), writes=()):
        self._waits(eng, reads, writes)
        ins = fn()
        eng.n += 1
        ins.then_inc(eng.sem, 1)
        for r in reads:
            r.r[eng] = eng.n
        for w in writes:
            w.w = {eng: eng.n}
            w.r = {}
        return ins

    def dma(self, q, src, out, in_, reads=(), writes=(), **kw):
        self._waits(q, reads, writes)
        ins = q.h.dma_start(out=out, in_=in_, **kw)
        src.n += 16
        ins.then_inc(src.sem, 16)
        for r in reads:
            r.r[src] = src.n
        for w in writes:
            w.w = {src: src.n}
            w.r = {}
        return ins


def build_program(n_layers=DEPTH, dbg=None):
    nc = bass.Bass("TRN2", target_bir_lowering=False)
    es = contextlib.ExitStack()
    k = K(nc, es)

    def dram(name, shape, dt=F32, kind="ExternalInput"):
        return nc.dram_tensor(name, list(shape), dt, kind=kind).ap()

    xT = dram("xT", [D, S])
    pk = dram("pk", [128, PK_COLS])
    w_ada = dram("w_ada", [DEPTH, D, 9 * D])
    ffn_gu = dram("ffn_gu", [DEPTH, 2, D, 2 * DFF])
    ffn_down = dram("ffn_down", [DEPTH, 2, DFF, D])
    gm_w_in = dram("gmlp_w_in", [N_A, D, GW])
    gm_w_out = dram("gmlp_w_out", [N_A, GH, D])
    gm_w_sT = dram("gmlp_w_sT", [N_A, 128, 8 * 128])
    gm_rows = dram("gm_rows", [N_A, 5, GH])
    w_ada_kv = dram("w_ada_kv", [D, 2 * D])
    w_kv = dram("w_kv", [D, 2 * D])
    w_q = dram("attn_w_q", [2, D, D])
    w_o = dram("attn_w_o", [2, D, D])
    biasT = dram("biasT", [2, NH, 5, 128, 128])
    ident = dram("ident", [128, 128])
    outT = dram("outT", [D, S], kind="ExternalOutput")
    import os as _os
    _kvkind = "ExternalOutput" if _os.environ.get("KDBG_KV") else "Internal"
    KTd = dram("KT_scr", [D, S], BF16, kind=_kvkind)
    Vd = dram("V_scr", [S, D], BF16, kind=_kvkind)

    sb = lambda name, shape, dt=F32: es.enter_context(nc.sbuf_tensor(name, list(shape), dt))

    Nt = sb("N", [128, NDT, S])
    Nres = [[Res() for _ in range(S // TT)] for _ in range(NDT)]
    hT = sb("hT", [128, NDT, 2 * TT], BF16)
    hres = [[Res() for _ in range(2)] for _ in range(NDT)]
    pkt = sb("pkt", [128, PK_COLS])
    pk_res = Res()
    wslots = [sb(f"wslot{i}", [128, SLOT], BF16) for i in range(NSLOT)]
    wres = [Res() for _ in range(NSLOT)]
    wsrc = [k.dsrc(f"s_w{i}") for i in range(NSLOT)]
    wstate = {"i": 0}
    NTMP = 1
    sq = [sb(f"sq{i}", [128, TT]) for i in range(2)]
    sq_res = [Res() for _ in range(2)]
    s1 = [sb(f"s1_{i}", [128, TT]) for i in range(2)]
    s2 = [sb(f"s2_{i}", [128, TT]) for i in range(2)]
    st_res = [Res() for _ in range(2)]
    rstd_t = sb("rstd_t", [128, TT])
    stat_res = Res()
    ones32 = sb("ones32", [128, 128])
    onesb = sb("onesb", [128, 128], BF16)
    identb = sb("identb", [128, 128], BF16)
    const_res = Res()
    cact = sb("cact", [128, NDT], BF16)
    cact_res = Res()
    modT = sb("modT", [128, 72])
    mod_res = Res()
    coef = sb("coef", [128, 3 * DEPTH + 1, 5, NDT])
    coef_res = [Res() for _ in range(3 * DEPTH + 1)]
    ctmp = sb("ctmp", [128, NDT])
    ctmp_res = Res()

    psum = [es.enter_context(nc.psum_tensor(f"ps{i}", [128, TT], F32)) for i in range(8)]
    pres = [Res() for _ in range(8)]
    pstate = {"i": 0}

    deferred = collections.deque()
    dstate = {"in": False}

    def bank():
        if not dstate["in"] and deferred and not dstate.get("pause"):
            dstate["in"] = True
            deferred.popleft()()
            dstate["in"] = False
        i = pstate["i"]
        pstate["i"] = (i + 1) % 7
        return psum[i], pres[i]

    def flush_deferred():
        dstate["in"] = True
        while deferred:
            deferred.popleft()()
        dstate["in"] = False

    mod_pb, mod_pr = psum[7], pres[7]
    hooks = collections.deque()
    hstate = {"in": False, "cnt": 0}

    mb_state = {"buf": None, "res": None}
    mb_src = k.dsrc("s_modbuf")

    def run_hook():
        if hstate["in"] or not hooks or mb_state["buf"] is None:
            return
        hstate["in"] = True
        hooks.popleft()()
        hstate["in"] = False

    def flush_hooks():
        hstate["in"] = True
        while hooks:
            hooks.popleft()()
        hstate["in"] = False

    def wload(view_elems, src_aps, rearr=None):
        i = wstate["i"]
        wstate["i"] = (i + 1) % NSLOT
        slot = wslots[i]
        a, b = view_elems
        view = slot[:, 0:a * b].rearrange("p (a b) -> p a b", a=a)
        for (a0, a1, b0, b1, ap) in src_aps:
            k.dma(k.pool, wsrc[i], view[:, a0:a1, b0:b1], ap, writes=[wres[i]], max_dma_last_dim=4096)
        run_hook()
        return view, wres[i]

    def pkcol(name, j=None):
        o, n = PK_OFF[name]
        if j is None:
            return pkt[:, o:o + n]
        return pkt[:, o + j:o + j + 1]

    s_in = k.dsrc("s_in")
    k.dma(k.sp, s_in, pkt[:], pk[:], writes=[pk_res])
    xv = xT.rearrange("(dt p) t -> p dt t", p=128)
    s_x = [k.dsrc(f"s_x{i}") for i in range(4)]
    for tt in range(S // TT):
        k.dma(k.sp, s_x[tt], Nt[:, :, tt * TT:(tt + 1) * TT], xv[:, :, tt * TT:(tt + 1) * TT],
              writes=[Nres[dt][tt] for dt in range(NDT)])
    k.op(k.dve, lambda: nc.vector.memset(ones32[:], 1.0), writes=[const_res])
    k.op(k.dve, lambda: nc.vector.memset(onesb[:], 1.0), writes=[const_res])
    ident_res = Res()
    k.dma(k.pool, k.dsrc("s_ident"), identb[:], ident[:], writes=[ident_res])
    k.op(k.act, lambda: nc.scalar.activation(out=cact[:], in_=pkcol("c"), func=AF.Silu),
         reads=[pk_res], writes=[cact_res])

    gb0 = sb("gb0", [128, 2, NDT])
    gb0_res = Res()
    k.op(k.dve, lambda: nc.vector.memset(gb0[:, 0, :], 1.0), writes=[gb0_res])
    k.op(k.dve, lambda: nc.vector.memset(gb0[:, 1, :], 0.0), writes=[gb0_res])

    MC = 512
    modkv = sb("modkv", [128, 16])
    modkv_res = Res()

    def mod_chunks(wsrc_ap, ncols, bias_ap, dst, dst_res, pcol0, split=None):
        wv = wsrc_ap.rearrange("(kt p) n -> p kt n", p=128)
        out = []

        def chunk(c0):
            if mb_state["buf"] is not None:
                view, vres = mb_state["buf"], mb_state["res"]
                k.dma(k.pool, mb_src, view[:, :, :], wv[:, :, c0:c0 + MC], writes=[vres], max_dma_last_dim=4096)
            else:
                view, vres = wload((NDT, MC), [(0, NDT, 0, MC, wv[:, :, c0:c0 + MC])])
            for jj in range(MC // 128):
                j = c0 // 128 + jj
                for kt in range(NDT):
                    k.op(k.pe, lambda j=j, kt=kt, jj=jj: nc.tensor.matmul(
                        mod_pb[:, pcol0 + j:pcol0 + j + 1], view[:, kt, jj * 128:(jj + 1) * 128], cact[:, kt:kt + 1],
                        start=(kt == 0), stop=(kt == NDT - 1), skip_group_check=True),
                        reads=[vres, cact_res], writes=[mod_pr])

        def final(j0, j1):
            k.op(k.dve, lambda: nc.vector.tensor_tensor(out=dst[:, j0:j1], in0=mod_pb[:, pcol0 + j0:pcol0 + j1],
                                                        in1=bias_ap[:, j0:j1], op=ALU.add),
                 reads=[mod_pr, pk_res], writes=[dst_res])

        nchunk = ncols // MC
        jpc = MC // 128
        for ic in range(nchunk):
            out.append(lambda c0=ic * MC: chunk(c0))
            if split is not None and ic == split - 1:
                out.append(lambda: final(0, split * jpc))
        out.append(lambda: final(0 if split is None else split * jpc, ncols // 128))
        return out

    def make_coef(ci, gprev, bprev, gb_res, ish, isc, igate, wgt, msrc=None, msrc_res=None):
        cr = coef_res[ci]
        c = lambda q: coef[:, ci, q, :]
        msrc = modT if msrc is None else msrc
        mod_res_ = mod_res if msrc_res is None else msrc_res
        md = lambda i: msrc[:, i * NDT:(i + 1) * NDT]
        V = nc.vector
        k.op(k.dve, lambda: V.tensor_scalar(out=ctmp[:], in0=md(isc), scalar1=1.0, scalar2=None, op0=ALU.add),
             reads=[mod_res_], writes=[ctmp_res])
        k.op(k.dve, lambda: V.tensor_tensor(out=c(0), in0=gprev, in1=ctmp[:], op=ALU.mult),
             reads=[ctmp_res, gb_res], writes=[cr])
        k.op(k.dve, lambda: V.tensor_tensor(out=c(1), in0=bprev, in1=ctmp[:], op=ALU.mult),
             reads=[ctmp_res, gb_res], writes=[cr])
        k.op(k.dve, lambda: V.tensor_tensor(out=c(1), in0=c(1), in1=md(ish), op=ALU.add),
             reads=[mod_res_, cr], writes=[cr])
        if igate is not None:
            k.op(k.dve, lambda: V.tensor_scalar(out=c(2), in0=gprev, scalar1=ALPHA, scalar2=None, op0=ALU.mult),
                 reads=[gb_res, cr], writes=[cr])
            k.op(k.dve, lambda: V.tensor_scalar(out=c(3), in0=bprev, scalar1=ALPHA, scalar2=None, op0=ALU.mult),
                 reads=[gb_res, cr], writes=[cr])
            k.op(k.dve, lambda: V.tensor_scalar(out=c(4), in0=md(igate), scalar1=wgt, scalar2=wgt,
                                                op0=ALU.mult, op1=ALU.add),
                 reads=[mod_res_, cr], writes=[cr])

    def modulate(ci, tts, hoff=0):
        for li, gt in enumerate(tts):
            for dt in range(NDT):
                k.op(k.act, lambda dt=dt, li=li, gt=gt: nc.scalar.activation(
                    out=hT[:, dt, (hoff + li) * TT:(hoff + li + 1) * TT], in_=Nt[:, dt, gt * TT:(gt + 1) * TT],
                    func=AF.Identity, scale=coef[:, ci, 0, dt:dt + 1], bias=coef[:, ci, 1, dt:dt + 1]),
                    reads=[Nres[dt][gt], coef_res[ci]], writes=[hres[dt][hoff + li]])

    def prescale(ci, tts):
        for gt in tts:
            for dt in range(NDT):
                k.op(k.act, lambda dt=dt, gt=gt: nc.scalar.activation(
                    out=Nt[:, dt, gt * TT:(gt + 1) * TT], in_=Nt[:, dt, gt * TT:(gt + 1) * TT],
                    func=AF.Identity, scale=coef[:, ci, 2, dt:dt + 1], bias=coef[:, ci, 3, dt:dt + 1]),
                    reads=[coef_res[ci]], writes=[Nres[dt][gt]])

    def epilogue(ci, dt, gt, pb, pr, si):
        nsl = Nt[:, dt, gt * TT:(gt + 1) * TT]
        V = nc.vector
        k.op(k.dve, lambda: V.scalar_tensor_tensor(out=nsl, in0=pb[:], scalar=coef[:, ci, 4, dt:dt + 1], in1=nsl,
                                                   op0=ALU.mult, op1=ALU.add),
             reads=[pr, coef_res[ci]], writes=[Nres[dt][gt]])
        if dt == 0:
            k.op(k.act, lambda: nc.scalar.activation(out=s2[si][:], in_=nsl, func=AF.Square),
                 reads=[Nres[dt][gt]], writes=[st_res[si]])
            k.op(k.dve, lambda: V.tensor_copy(out=s1[si][:], in_=nsl), reads=[Nres[dt][gt]], writes=[st_res[si]])
        else:
            qi = dt % 2
            k.op(k.act, lambda: nc.scalar.activation(out=sq[qi][:], in_=nsl, func=AF.Square),
                 reads=[Nres[dt][gt]], writes=[sq_res[qi]])
            k.op(k.dve, lambda: V.tensor_tensor(out=s1[si][:], in0=s1[si][:], in1=nsl, op=ALU.add),
                 reads=[Nres[dt][gt]], writes=[st_res[si]])
            k.op(k.dve, lambda: V.tensor_tensor(out=s2[si][:], in0=s2[si][:], in1=sq[qi][:], op=ALU.add),
                 reads=[sq_res[qi]], writes=[st_res[si]])

    def ln_finish(gt, si):
        V = nc.vector

        mean_t, nmr_t = s1[si], s2[si]

        def c1():
            p1, r1 = bank()
            p2, r2 = bank()
            k.op(k.pe, lambda: nc.tensor.matmul(p1[:], ones32[:], s1[si][:], start=True, stop=True),
                 reads=[st_res[si], const_res], writes=[r1])
            k.op(k.pe, lambda: nc.tensor.matmul(p2[:], ones32[:], s2[si][:], start=True, stop=True),
                 reads=[st_res[si], const_res], writes=[r2])
            k.op(k.dve, lambda: V.tensor_scalar(out=mean_t[:], in0=p1[:], scalar1=1.0 / D, scalar2=None, op0=ALU.mult),
                 reads=[r1], writes=[st_res[si]])
            k.op(k.act, lambda: nc.scalar.activation(out=nmr_t[:], in_=mean_t[:], func=AF.Square),
                 reads=[st_res[si]], writes=[st_res[si]])
            k.op(k.dve, lambda: V.scalar_tensor_tensor(out=rstd_t[:], in0=p2[:], scalar=1.0 / D, in1=nmr_t[:],
                                                       op0=ALU.mult, op1=ALU.subtract),
                 reads=[r2, st_res[si]], writes=[stat_res])

        def c2():
            pass

        def c3():
            k.op(k.dve, lambda: V.tensor_scalar(out=rstd_t[:], in0=rstd_t[:], scalar1=EPS, scalar2=None, op0=ALU.add),
                 reads=[stat_res], writes=[stat_res])
            k.op(k.act, lambda: nc.scalar.activation(out=rstd_t[:], in_=rstd_t[:], func=AF.Sqrt),
                 reads=[stat_res], writes=[stat_res])

        def c4():
            k.op(k.dve, lambda: V.reciprocal(out=rstd_t[:], in_=rstd_t[:]),
                 reads=[stat_res], writes=[stat_res])
            k.op(k.dve, lambda: V.scalar_tensor_tensor(out=nmr_t[:], in0=mean_t[:], scalar=-1.0, in1=rstd_t[:],
                                                       op0=ALU.mult, op1=ALU.mult),
                 reads=[stat_res, st_res[si]], writes=[st_res[si]])

        def cn(dt):
            nsl = Nt[:, dt, gt * TT:(gt + 1) * TT]
            k.op(k.dve, lambda: V.tensor_tensor(out=nsl, in0=nsl, in1=rstd_t[:], op=ALU.mult),
                 reads=[stat_res], writes=[Nres[dt][gt]])
            k.op(k.dve, lambda: V.tensor_tensor(out=nsl, in0=nsl, in1=nmr_t[:], op=ALU.add),
                 reads=[st_res[si]], writes=[Nres[dt][gt]])

        deferred.extend([c1, c2, c3, c4] + [(lambda dt=dt: cn(dt)) for dt in range(NDT)])

    def ffn(ci, l, which):
        wg = ffn_gu[l, which].rearrange("(kt p) n -> p kt n", p=128)
        wd = ffn_down[l, which].rearrange("(fc p) n -> p fc n", p=128)
        fen = k.fence()
        with nc.sbuf_tensor(f"hid{ci}", [128, NFC, 2 * TT], BF16) as hid, \
                nc.sbuf_tensor(f"sg{ci}", [128, NTMP, TT], F32) as sgt, \
                nc.sbuf_tensor(f"modbuf{ci}", [128, NDT, MC], BF16) as mbt:
            mb_state["buf"], mb_state["res"] = mbt, Res(fen)
            hid_res = [[Res(fen) for _ in range(2)] for _ in range(NFC)]
            sg = [sgt[:, i, :] for i in range(NTMP)]
            sg_res = [Res(fen) for _ in range(NTMP)]
            for half in range(2):
                tts = [2 * half, 2 * half + 1]
                for f0 in range(0, NFC, 2):
                    view, vres = wload((NDT, 512), [
                        (0, NDT, 0, 256, wg[:, :, f0 * 128:f0 * 128 + 256]),
                        (0, NDT, 256, 512, wg[:, :, DFF + f0 * 128:DFF + f0 * 128 + 256])])
                    for fi in range(2):
                        fc = f0 + fi
                        bg = [bank() for _ in range(2)]
                        bu = [bank() for _ in range(2)]
                        for (bks, coff) in ((bg, fi * 128), (bu, 256 + fi * 128)):
                            for kt in range(NDT):
                                for li in range(2):
                                    pb, pr = bks[li]
                                    k.op(k.pe, lambda pb=pb, kt=kt, li=li, coff=coff: nc.tensor.matmul(
                                        pb[:], view[:, kt, coff:coff + 128], hT[:, kt, li * TT:(li + 1) * TT],
                                        start=(kt == 0), stop=(kt == NDT - 1)),
                                        reads=[vres, hres[kt][li]], writes=[pr])
                        for li in range(2):
                            ti = (fc * 2 + li) % NTMP
                            pg, rg = bg[li]
                            pu, ru = bu[li]
                            k.op(k.act, lambda pg=pg, ti=ti: nc.scalar.activation(out=sg[ti], in_=pg[:], func=AF.Silu),
                                 reads=[rg], writes=[sg_res[ti]])
                            k.op(k.dve, lambda pu=pu, ti=ti, fc=fc, li=li: nc.vector.tensor_tensor(
                                out=hid[:, fc, li * TT:(li + 1) * TT], in0=pu[:], in1=sg[ti], op=ALU.mult),
                                reads=[ru, sg_res[ti]], writes=[hid_res[fc][li]])
                yield
                for d0 in range(0, NDT, 2):
                    view, vres = wload((NFC, 256), [(0, NFC, 0, 256, wd[:, :, d0 * 128:d0 * 128 + 256])])
                    for di in range(2):
                        dt = d0 + di
                        for li in range(2):
                            pb, pr = bank()
                            for fc in range(NFC):
                                k.op(k.pe, lambda pb=pb, fc=fc, li=li, di=di: nc.tensor.matmul(
                                    pb[:], view[:, fc, di * 128:(di + 1) * 128], hid[:, fc, li * TT:(li + 1) * TT],
                                    start=(fc == 0), stop=(fc == NFC - 1)),
                                    reads=[vres, hid_res[fc][li]], writes=[pr])
                            epilogue(ci, dt, tts[li], pb, pr, li)
                for li in range(2):
                    ln_finish(tts[li], li)
            flush_hooks()
            mb_state["buf"], mb_state["res"] = None, None

    def gmlp(ci, l):
        V = nc.vector
        w_in = gm_w_in[l].rearrange("(kt p) n -> p kt n", p=128)
        w_out = gm_w_out[l].rearrange("(cc p) n -> p cc n", p=128)
        fen = k.fence()
        with contextlib.ExitStack() as gs:
            gsb = lambda name, shape, dt=F32: gs.enter_context(nc.sbuf_tensor(f"{name}_{l}", list(shape), dt))
            u = gsb("g_u", [128, 16, TT], BF16)
            u_res = [Res(fen) for _ in range(16)]
            vpre = gsb("g_vpre", [128, 4, GH])
            vpre_res = [[Res(fen) for _ in range(4)] for _ in range(4)]
            vb = gsb("g_vb", [128, 2, GH], BF16)
            vb_res = [Res(fen) for _ in range(2)]
            gbc = gsb("g_gbc", [128, GH])
            wsT = gsb("g_wsT", [128, 1024], BF16)
            Lr = gsb("g_L", [128, GH], BF16)
            Rr = gsb("g_R", [128, 1024], BF16)
            smal = gsb("g_small", [128, 4, 4 * 6 + 2 + 2])
            smal_res = [Res(fen) for _ in range(4)]
            gbc_res, ws_res, L_res, R_res, r64_res, rtmp_res = (Res(fen) for _ in range(6))
            sg_ = [k.dsrc(f"s_gm{l}_{i}") for i in range(9)]
            k.dma(k.sp, sg_[0], gbc[:], gm_rows[l, 1:2, :].partition_broadcast(128), writes=[gbc_res])
            k.dma(k.sp, sg_[1], sq[0][32:33, :], gm_rows[l, 3:4, 0:512], writes=[sq_res[0]])
            k.dma(k.sp, sg_[5 + 3 - 1], sq[1][32:33, :], gm_rows[l, 3:4, 512:1024], writes=[sq_res[1]])
            k.dma(k.pool, sg_[2], Lr[0:1, :], gm_rows[l, 2:3, :], writes=[L_res], max_dma_last_dim=4096)
            k.dma(k.pool, sg_[3], Lr[64:65, :], gm_rows[l, 0:1, :], writes=[r64_res], max_dma_last_dim=4096)
            k.dma(k.pool, sg_[4], wsT[:], gm_w_sT[l], writes=[ws_res], max_dma_last_dim=4096)
            L12_res = [Res(fen), Res(fen)]
            sg_ones = [k.dsrc(f"s_gm{l}_ones{i}") for i in range(2)]
            for rr in (1, 2):
                k.dma(k.pool, sg_ones[rr - 1], Lr[rr:rr + 1, :], gm_rows[l, 4:5, :], writes=[L12_res[rr - 1]],
                      max_dma_last_dim=4096)
            wsT3 = wsT[:].rearrange("p (g t) -> p g t", g=8)
            k.op(k.dve, lambda: V.memset(wsT3[64:128, :, 0:64], 0.0), writes=[ws_res])
            for hh in range(2):
                pb, pr = bank()
                k.op(k.pe, lambda pb=pb, hh=hh: nc.tensor.matmul(pb[0:1, :], onesb[:, 0:1], wsT[:, hh * 512:(hh + 1) * 512],
                                                                start=True, stop=True),
                     reads=[ws_res, const_res], writes=[pr])
                k.op(k.dve, lambda pb=pb, hh=hh: V.tensor_copy(out=Rr[0:1, hh * 512:(hh + 1) * 512], in_=pb[0:1, :]),
                     reads=[pr], writes=[R_res])
            t32 = Rr[32:33, :]
            for hh, src_t in ((0, sq[0]), (1, sq[1])):
                k.op(k.dve, lambda hh=hh, src_t=src_t: V.tensor_copy(out=Rr[32:33, hh * 512:(hh + 1) * 512], in_=src_t[32:33, :]),
                     reads=[sq_res[0], sq_res[1]], writes=[rtmp_res])
            k.dma(k.sp, sg_[6], Rr[1:2, :], t32, reads=[rtmp_res], writes=[R_res])
            for hh, src_t in ((0, sq[0]), (1, sq[1])):
                k.op(k.dve, lambda hh=hh, src_t=src_t: V.tensor_tensor(
                    out=Rr[32:33, hh * 512:(hh + 1) * 512], in0=src_t[32:33, :], in1=Rr[32:33, hh * 512:(hh + 1) * 512],
                    op=ALU.subtract),
                    reads=[sq_res[0], sq_res[1]], writes=[rtmp_res])
            k.dma(k.sp, sg_[8], Rr[2:3, :], t32, reads=[rtmp_res], writes=[R_res])

            for p in range(S // TT):
                gt = p
                ho = p % 2
                si = p % 2
                hsl = lambda kt, a, b, ho=ho: hT[:, kt, ho * TT + a:ho * TT + b]

                def uphase():
                    for c0 in range(0, 16, 4):
                        view, vres = wload((NDT, 512), [(0, NDT, 0, 512, w_in[:, :, c0 * 128:c0 * 128 + 512])])
                        for cj in range(4):
                            cc = c0 + cj
                            pb, pr = bank()
                            for kt in range(NDT):
                                k.op(k.pe, lambda pb=pb, kt=kt, cj=cj: nc.tensor.matmul(
                                    pb[:], view[:, kt, cj * 128:(cj + 1) * 128], hsl(kt, 0, TT),
                                    start=(kt == 0), stop=(kt == NDT - 1)),
                                    reads=[vres, hres[kt][ho]], writes=[pr])
                            o, _ = PK_OFF["b_in_u"]
                            bcol = pkt[:, o + l * 16 + cc:o + l * 16 + cc + 1]
                            k.op(k.act, lambda pb=pb, cc=cc, bcol=bcol: nc.scalar.activation(
                                out=u[:, cc, :], in_=pb[:], func=AF.Gelu, bias=bcol),
                                reads=[pr, pk_res], writes=[u_res[cc]])

                def vphase():
                    for vc in range(4):
                        view, vres = wload((NDT, 512), [(0, NDT, 0, 512, w_in[:, :, GH + vc * 512:GH + (vc + 1) * 512])])
                        for w in range(4):
                            pb, pr = bank()
                            for kt in range(NDT):
                                k.op(k.pe, lambda pb=pb, kt=kt, w=w: nc.tensor.matmul(
                                    pb[:], hsl(kt, w * 128, (w + 1) * 128), view[:, kt, :],
                                    start=(kt == 0), stop=False),
                                    reads=[vres, hres[kt][ho]], writes=[pr])
                            k.op(k.pe, lambda pb=pb, vc=vc: nc.tensor.matmul(
                                pb[:], onesb[64:65, :], Lr[64:65, vc * 512:(vc + 1) * 512], start=False, stop=True),
                                reads=[r64_res, const_res], writes=[pr])
                            k.op(k.act, lambda pb=pb, w=w, vc=vc: nc.scalar.activation(
                                out=vpre[:, w, vc * 512:(vc + 1) * 512], in_=pb[:], func=AF.Gelu),
                                reads=[pr], writes=[vpre_res[w][vc]])

                def stats_all():
                    for w in range(4):
                        sm = smal[:, w, :]
                        for vc in range(4):
                            k.op(k.dve, lambda vc=vc, w=w, sm=sm: V.bn_stats(
                                out=sm[:, vc * 6:(vc + 1) * 6], in_=vpre[:, w, vc * 512:(vc + 1) * 512]),
                                reads=[vpre_res[w][vc]], writes=[smal_res[0]])
                        k.op(k.dve, lambda sm=sm: V.bn_aggr(out=sm[:, 24:26], in_=sm[:, 0:24]),
                             reads=[smal_res[0]], writes=[smal_res[0]])
                    var4, rs4, mu4, nb4 = smal[:, :, 25], smal[:, :, 26], smal[:, :, 24], smal[:, :, 27]
                    k.op(k.dve, lambda: V.tensor_scalar(out=rs4, in0=var4, scalar1=EPS, scalar2=None, op0=ALU.add),
                         reads=[smal_res[0]], writes=[smal_res[0]])
                    k.op(k.act, lambda: nc.scalar.activation(out=rs4, in_=rs4, func=AF.Sqrt),
                         reads=[smal_res[0]], writes=[smal_res[0]])
                    k.op(k.dve, lambda: V.reciprocal(out=rs4, in_=rs4), reads=[smal_res[0]], writes=[smal_res[0]])
                    k.op(k.dve, lambda: V.scalar_tensor_tensor(out=nb4, in0=mu4, scalar=-1.0, in1=rs4,
                                                               op0=ALU.mult, op1=ALU.mult),
                         reads=[smal_res[0]], writes=[smal_res[0]])

                def norm(w):
                    sm = smal[:, w, :]
                    k.op(k.act, lambda: nc.scalar.activation(
                        out=vpre[:, w, :], in_=vpre[:, w, :], func=AF.Identity, scale=sm[:, 26:27], bias=sm[:, 27:28]),
                        reads=[smal_res[0]], writes=vpre_res[w])

                def vbmul(w):
                    wi = w % 2
                    k.op(k.dve, lambda: V.tensor_tensor(out=vb[:, wi, :], in0=vpre[:, w, :], in1=gbc[:], op=ALU.mult),
                         reads=vpre_res[w] + [gbc_res], writes=[vb_res[wi]])

                def spatial(wi, w):
                    for cg in range(4):
                        pb, pr = bank()
                        for cj in range(4):
                            cc = cg * 4 + cj
                            g = cc // 2
                            k.op(k.pe, lambda pb=pb, cj=cj, cc=cc, g=g: nc.tensor.matmul(
                                pb[:, cj * 128:(cj + 1) * 128], vb[:, wi, cc * 128:(cc + 1) * 128],
                                wsT[:, g * 128:(g + 1) * 128], start=(cj == 0), stop=False, skip_group_check=True),
                                reads=[vb_res[wi], ws_res], writes=[pr])
                        for cj in range(4):
                            cc = cg * 4 + cj
                            g = cc // 2
                            k.op(k.pe, lambda pb=pb, cj=cj, cc=cc, g=g: nc.tensor.matmul(
                                pb[:, cj * 128:(cj + 1) * 128], Lr[0:3, cc * 128:(cc + 1) * 128],
                                Rr[0:3, g * 128:(g + 1) * 128], start=False, stop=True, skip_group_check=True),
                                reads=[L_res, R_res] + L12_res, writes=[pr])
                        usl = u[:, cg * 4:(cg + 1) * 4, w * 128:(w + 1) * 128]
                        k.op(k.dve, lambda pb=pb, usl=usl: V.tensor_tensor(
                            out=usl, in0=pb[:].rearrange("p (a b) -> p a b", a=4), in1=usl, op=ALU.mult),
                            reads=[pr], writes=u_res[cg * 4:(cg + 1) * 4])

                dstate["pause"] = True
                vphase()
                stats_all()
                norm(0)
                vbmul(0)
                norm(1)
                vbmul(1)
                norm(2)
                norm(3)
                dstate["pause"] = False
                uphase()
                spatial(0, 0)
                vbmul(2)
                spatial(1, 1)
                vbmul(3)
                spatial(0, 2)
                spatial(1, 3)
                yield
                for d0 in range(0, NDT, 2):
                    view, vres = wload((16, 256), [(0, 16, 0, 256, w_out[:, :, d0 * 128:d0 * 128 + 256])])
                    for di in range(2):
                        dt = d0 + di
                        pb, pr = bank()
                        for cc in range(16):
                            k.op(k.pe, lambda pb=pb, cc=cc, di=di: nc.tensor.matmul(
                                pb[:], view[:, cc, di * 128:(di + 1) * 128], u[:, cc, :],
                                start=(cc == 0), stop=(cc == 15)),
                                reads=[vres, u_res[cc]], writes=[pr])
                        epilogue(ci, dt, gt, pb, pr, si)
                ln_finish(gt, si)

    kv_res = Res()
    s_kvst = [k.dsrc(f"s_kvst{i}") for i in range(4)]

    def kvload(view_elems, out_in_pairs):
        i = wstate["i"]
        wstate["i"] = (i + 1) % NSLOT
        a, b = view_elems
        view = wslots[i][:, 0:a * b].rearrange("p (a b) -> p a b", a=a)
        for (sl, ap) in out_in_pairs:
            k.dma(k.pool, wsrc[i], sl(view), ap, reads=[kv_res], writes=[wres[i]])
        return view, wres[i]

    def kv_produce(ci):
        KTv = KTd.rearrange("(dt p) s -> p dt s", p=128)
        wkv = w_kv.rearrange("(kt p) n -> p kt n", p=128)
        fen = k.fence()
        with nc.sbuf_tensor("kvst", [128, 4, TT], BF16) as kvst:
            st_r = [Res(fen) for _ in range(4)]
            cnt = [0]

            def evac_store(pb, pr, dst):
                i = cnt[0] % 4
                cnt[0] += 1
                if i % 2 == 0:
                    k.op(k.act, lambda: nc.scalar.copy(out=kvst[:, i, :], in_=pb[:]), reads=[pr], writes=[st_r[i]])
                else:
                    k.op(k.dve, lambda: nc.vector.tensor_copy(out=kvst[:, i, :], in_=pb[:]), reads=[pr], writes=[st_r[i]])
                k.dma(k.sp, s_kvst[i], dst, kvst[:, i, :], reads=[st_r[i]])

            for half in range(2):
                tts = [2 * half, 2 * half + 1]
                for c0 in (0, 512):
                    view, vres = wload((NDT, 512), [(0, NDT, 0, 512, wkv[:, :, c0:c0 + 512])])
                    for dj in range(4):
                        dt = c0 // 128 + dj
                        for li in range(2):
                            pb, pr = bank()
                            for kt in range(NDT):
                                k.op(k.pe, lambda pb=pb, kt=kt, dj=dj, li=li: nc.tensor.matmul(
                                    pb[:], view[:, kt, dj * 128:(dj + 1) * 128], hT[:, kt, li * TT:(li + 1) * TT],
                                    start=(kt == 0), stop=(kt == NDT - 1)),
                                    reads=[vres, hres[kt][li]], writes=[pr])
                            evac_store(pb, pr, KTv[:, dt, tts[li] * TT:(tts[li] + 1) * TT])
                for c0 in (0, 512):
                    view, vres = wload((NDT, 512), [(0, NDT, 0, 512, wkv[:, :, D + c0:D + c0 + 512])])
                    for tb in range(8):
                        li = tb // 4
                        tok0 = half * 1024 + tb * 128
                        pb, pr = bank()
                        for kt in range(NDT):
                            k.op(k.pe, lambda pb=pb, kt=kt, tb=tb: nc.tensor.matmul(
                                pb[:], hT[:, kt, tb * 128:(tb + 1) * 128], view[:, kt, :],
                                start=(kt == 0), stop=(kt == NDT - 1)),
                                reads=[vres, hres[kt][li]], writes=[pr])
                        evac_store(pb, pr, Vd[tok0:tok0 + 128, c0:c0 + 512])
                if half == 1:
                    kv_res.w = {sx: sx.n for sx in s_kvst}
                yield

    bias_state = {"buf": None, "res": Res()}

    def bias_hooks(l):
        j_l = l - N_A
        if bias_state["buf"] is None:
            bias_state["buf"] = sb("a_bias", [128, NH * 5, 128], BF16)
            bias_state["res"] = Res(k.fence())
        biasb, bias_res = bias_state["buf"], bias_state["res"]
        bsrc = biasT[j_l].rearrange("h j r t -> r (h j) t")
        s_b = [k.dsrc(f"s_bias{l}_{i}") for i in range(8)]
        piece_res = [Res() for _ in range(8)]
        b4 = biasb[:].rearrange("p (h j) t -> p h j t", j=5)
        out = []

        def piece(i):
            h0 = 2 * i
            k.dma(k.pool, s_b[i], biasb[:, h0 * 5:(h0 + 2) * 5, :], bsrc[:, h0 * 5:(h0 + 2) * 5, :],
                  reads=[], writes=[piece_res[i], bias_res] if i == 0 else [piece_res[i]])

        def masks():
            k.op(k.dve, lambda: nc.vector.memset(b4[64:128, :, 4, 0:64], NEG), reads=piece_res, writes=[bias_res])
            k.op(k.dve, lambda: nc.vector.memset(b4[0:64, :, 0, 64:128], NEG), writes=[bias_res])

        for i in range(8):
            out.append(lambda i=i: piece(i))
        out.append(masks)
        return out

    def attention(ci, l):
        V = nc.vector
        j_l = l - N_A
        wq = w_q[j_l].rearrange("(kt p) n -> p kt n", p=128)
        wo = w_o[j_l].rearrange("(kt p) n -> p kt n", p=128)
        KTv = KTd.rearrange("(dt p) s -> p dt s", p=128)
        Vv = Vd.rearrange("(kt p) c -> p kt c", p=128)
        fen = k.fence()
        with contextlib.ExitStack() as gs:
            gsb = lambda name, shape, dt=F32: gs.enter_context(nc.sbuf_tensor(f"{name}_{l}", list(shape), dt))
            qT = gsb("a_qT", [128, NDT, TT], BF16)
            q_res = [Res(fen) for _ in range(NDT)]
            oT = gsb("a_oT", [128, NDT, TT], BF16)
            o_res = [Res(fen) for _ in range(NDT)]
            biasb, bias_res = bias_state["buf"], bias_state["res"]
            NPT = 2
            PT = [gsb(f"a_PT{i}", [128, 5, 1024], BF16) for i in range(NPT)]
            pt_res = [[Res(fen) for _ in range(5)] for _ in range(NPT)]
            rinv = [gsb(f"a_rinv{i}", [128, TT]) for i in range(2)]
            rinv_res = [Res(fen) for _ in range(2)]
            for qp in range(S // TT):
                gt = qp
                ho = qp % 2
                si = qp % 2
                for c0 in (0, 512):
                    view, vres = wload((NDT, 512), [(0, NDT, 0, 512, wq[:, :, c0:c0 + 512])])
                    for dj in range(4):
                        dt = c0 // 128 + dj
                        pb, pr = bank()
                        for kt in range(NDT):
                            k.op(k.pe, lambda pb=pb, kt=kt, dj=dj: nc.tensor.matmul(
                                pb[:], view[:, kt, dj * 128:(dj + 1) * 128], hT[:, kt, ho * TT:(ho + 1) * TT],
                                start=(kt == 0), stop=(kt == NDT - 1)),
                                reads=[vres, hres[kt][ho]], writes=[pr])
                        k.op(k.act, lambda pb=pb, dt=dt: nc.scalar.activation(
                            out=qT[:, dt, :], in_=pb[:], func=AF.Copy, scale=HD ** -0.5),
                            reads=[pr], writes=[q_res[dt]])
                band = {}

                def qk(ku):
                    m, hg = ku // 2, ku % 2
                    pm = 4 * qp + m
                    nj = min(5, pm + 1)
                    j0 = 5 - nj
                    kt0 = pm - 4 + j0
                    if hg == 0:
                        Kb, kres = kvload((NDT, nj * 128), [(lambda v: v[:, :, :], KTv[:, :, kt0 * 128:(pm + 1) * 128])])
                        Vb, vres_ = kvload((nj, D), [(lambda v: v[:, :, :], Vv[:, kt0:pm + 1, :])])
                        band[m] = (Kb, kres, Vb, vres_)
                    Kb, kres, Vb, vres_ = band[m]
                    qs = slice(m * 128, (m + 1) * 128)
                    pi = ku % NPT
                    P = PT[pi]
                    for jj in range(nj):
                        j = j0 + jj
                        bA = bank()
                        bB = bank()
                        for di in range(4):
                            dt = hg * 4 + di
                            for hh, (pb, pr) in ((0, bA), (1, bB)):
                                k.op(k.pe, lambda pb=pb, dt=dt, hh=hh, di=di, jj=jj: nc.tensor.matmul(
                                    pb[:, di * 128:(di + 1) * 128],
                                    Kb[hh * 64:(hh + 1) * 64, dt, jj * 128:(jj + 1) * 128],
                                    qT[hh * 64:(hh + 1) * 64, dt, qs], start=(di == 0), stop=False,
                                    skip_group_check=True),
                                    reads=[kres, q_res[dt]], writes=[pr])
                        for hh, (pb, pr) in ((0, bA), (1, bB)):
                            for di in range(4):
                                h = (hg * 4 + di) * 2 + hh
                                k.op(k.pe, lambda pb=pb, di=di, h=h, j=j: nc.tensor.matmul(
                                    pb[:, di * 128:(di + 1) * 128], identb[:], biasb[:, h * 5 + j, :],
                                    start=False, stop=True, skip_group_check=True),
                                    reads=[bias_res, ident_res], writes=[pr])
                            k.op(k.act, lambda pb=pb, hh=hh, jj=jj, P=P: nc.scalar.activation(
                                out=P[:, jj, hh * 512:(hh + 1) * 512], in_=pb[:], func=AF.Exp),
                                reads=[pr], writes=[pt_res[pi][jj]])

                def spv(ku):
                    m, hg = ku // 2, ku % 2
                    pm = 4 * qp + m
                    nj = min(5, pm + 1)
                    Kb, kres, Vb, vres_ = band[m]
                    qs = slice(m * 128, (m + 1) * 128)
                    pi = ku % NPT
                    P = PT[pi]
                    sbk, sres = bank()
                    obk, ores = bank()
                    for jj in range(nj):
                        for hh in range(2):
                            k.op(k.pe, lambda hh=hh, jj=jj: nc.tensor.matmul(
                                sbk[hh * 64:(hh + 1) * 64, :], onesb[:, 0:64], P[:, jj, hh * 512:(hh + 1) * 512],
                                start=(jj == 0), stop=(jj == nj - 1), skip_group_check=True),
                                reads=[pt_res[pi][jj], const_res], writes=[sres])
                    for jj in range(nj):
                        for di in range(4):
                            dt = hg * 4 + di
                            for hh in range(2):
                                k.op(k.pe, lambda dt=dt, hh=hh, di=di, jj=jj: nc.tensor.matmul(
                                    obk[hh * 64:(hh + 1) * 64, di * 128:(di + 1) * 128],
                                    Vb[:, jj, dt * 128 + hh * 64:dt * 128 + (hh + 1) * 64],
                                    P[:, jj, hh * 512 + di * 128:hh * 512 + (di + 1) * 128],
                                    start=(jj == 0 and di == 0), stop=(jj == nj - 1), skip_group_check=True),
                                    reads=[vres_, pt_res[pi][jj]], writes=[ores])
                    ri = ku % 2
                    k.op(k.dve, lambda: V.reciprocal(out=rinv[ri][:], in_=sbk[:]),
                         reads=[sres], writes=[rinv_res[ri]])
                    k.op(k.dve, lambda: V.tensor_tensor(
                        out=oT[:, hg * 4:(hg + 1) * 4, qs],
                        in0=obk[:].rearrange("p (a b) -> p a b", a=4),
                        in1=rinv[ri][:].rearrange("p (a b) -> p a b", a=4), op=ALU.mult),
                        reads=[ores, rinv_res[ri]], writes=o_res[hg * 4:(hg + 1) * 4])

                qk(0)
                for ku in range(8):
                    if ku + 1 < 8:
                        qk(ku + 1)
                    spv(ku)
                    if ku < 7:
                        yield 2
                yield
                for c0 in (0, 512):
                    view, vres = wload((NDT, 512), [(0, NDT, 0, 512, wo[:, :, c0:c0 + 512])])
                    for dj in range(4):
                        dt = c0 // 128 + dj
                        pb, pr = bank()
                        for kt in range(NDT):
                            k.op(k.pe, lambda pb=pb, kt=kt, dj=dj: nc.tensor.matmul(
                                pb[:], view[:, kt, dj * 128:(dj + 1) * 128], oT[:, kt, :],
                                start=(kt == 0), stop=(kt == NDT - 1)),
                                reads=[vres, o_res[kt]], writes=[pr])
                        epilogue(ci, dt, gt, pb, pr, si)
                ln_finish(gt, si)

    def pro(ci, tts, hoff=0, scale=True):
        pieces = []
        for li, gt in enumerate(tts):
            for dt in range(NDT):
                pieces.append(lambda dt=dt, li=li, gt=gt: k.op(k.act, lambda: nc.scalar.activation(
                    out=hT[:, dt, (hoff + li) * TT:(hoff + li + 1) * TT], in_=Nt[:, dt, gt * TT:(gt + 1) * TT],
                    func=AF.Identity, scale=coef[:, ci, 0, dt:dt + 1], bias=coef[:, ci, 1, dt:dt + 1]),
                    reads=[Nres[dt][gt], coef_res[ci]], writes=[hres[dt][hoff + li]]))
        if scale:
            for gt in tts:
                for dt in range(NDT):
                    pieces.append(lambda dt=dt, gt=gt: k.op(k.act, lambda: nc.scalar.activation(
                        out=Nt[:, dt, gt * TT:(gt + 1) * TT], in_=Nt[:, dt, gt * TT:(gt + 1) * TT],
                        func=AF.Identity, scale=coef[:, ci, 2, dt:dt + 1], bias=coef[:, ci, 3, dt:dt + 1]),
                        reads=[coef_res[ci]], writes=[Nres[dt][gt]]))
        return pieces

    def ln_affine(l, i):
        o, _ = PK_OFF["ln"]
        base = o + ((l * 3 + i) * 2) * NDT
        return pkt[:, base:base + NDT], pkt[:, base + NDT:base + 2 * NDT], pk_res

    def prev_affine(l, i):
        if l == 0 and i == 0:
            return gb0[:, 0, :], gb0[:, 1, :], gb0_res
        return ln_affine(l, i - 1) if i > 0 else ln_affine(l - 1, 2)

    def coefs(l, ii):
        for i in ii:
            g_, b_, r_ = prev_affine(l, i)
            make_coef(3 * l + i, g_, b_, r_, 3 * i, 3 * i + 1, 3 * i + 2, 0.5 if i != 1 else 1.0)

    def layer_mod_hooks(l, split=None):
        return mod_chunks(w_ada[l], 9 * D, pkcol("b_ada")[:, l * 72:(l + 1) * 72], modT, mod_res, 0, split=split)

    def kv_mod_hooks():
        return mod_chunks(w_ada_kv, 2 * D, pkcol("b_ada_kv"), modkv, modkv_res, 72)

    plan = []
    for l in range(n_layers):
        for i in range(3):
            ci = 3 * l + i
            pre = None
            post = None
            if i == 0 and l > 0:
                pre = (lambda l=l: (flush_hooks(), coefs(l, (0, 1, 2))))
            if l == 0 and i == 1:
                pre = (lambda: (flush_hooks(), coefs(0, (1, 2))))
            if i != 1:
                pros = [pro(ci, [0, 1]), pro(ci, [2, 3])]
                gen = (lambda ci=ci, l=l, i=i: ffn(ci, l, 0 if i == 0 else 1))
            elif l < N_A:
                pros = [pro(ci, [p], hoff=p % 2) for p in range(4)]
                gen = (lambda ci=ci, l=l: gmlp(ci, l))
            else:
                pros = [pro(ci, [p], hoff=p % 2) for p in range(4)]
                gen = (lambda ci=ci, l=l: attention(ci, l))
            start = None
            if i == 0 and l >= N_A:
                def start(l=l):
                    hooks.extend(bias_hooks(l))
            if i == 1 and l + 1 < n_layers:
                def start(l=l):
                    if l == N_A - 1:
                        hooks.extend(kv_mod_hooks())
                    hooks.extend(layer_mod_hooks(l + 1))
            plan.append([pre, pros, gen, start])
        if l == N_A - 1 and n_layers > N_A:
            def kvpre(l=l):
                flush_hooks()
                g_, b_, r_ = ln_affine(l, 2)
                make_coef(3 * DEPTH, g_, b_, r_, 0, 1, None, None, msrc=modkv, msrc_res=modkv_res)
            plan.append([kvpre, [pro(3 * DEPTH, [0, 1], scale=False), pro(3 * DEPTH, [2, 3], scale=False)],
                         (lambda: kv_produce(3 * DEPTH)), None])

    m0 = layer_mod_hooks(0, split=6)
    for f in m0[:7]:
        f()
    hooks.extend(m0[7:])
    coefs(0, (0,))

    ov = outT.rearrange("(dt p) t -> p dt t", p=128)
    s_out = k.dsrc("s_out")
    ctmp2 = sb("ctmp2", [128, 2, NDT])
    c2_res = Res()
    gfin, bfin, gbfin_res = ln_affine(n_layers - 1, 2)
    k.op(k.dve, lambda: nc.vector.tensor_copy(out=ctmp2[:, 0, :], in_=gfin), reads=[gbfin_res], writes=[c2_res])
    k.op(k.dve, lambda: nc.vector.tensor_copy(out=ctmp2[:, 1, :], in_=bfin), reads=[gbfin_res, c2_res], writes=[c2_res])

    def out_stage(gts):
        for gt in gts:
            for dt in range(NDT):
                nsl = Nt[:, dt, gt * TT:(gt + 1) * TT]
                k.op(k.act, lambda nsl=nsl, dt=dt: nc.scalar.activation(
                    out=nsl, in_=nsl, func=AF.Identity, scale=ctmp2[:, 0, dt:dt + 1], bias=ctmp2[:, 1, dt:dt + 1]),
                    reads=[c2_res], writes=[Nres[dt][gt]])
                k.dma(k.sp, s_out, ov[:, dt, gt * TT:(gt + 1) * TT], nsl, reads=[Nres[dt][gt]])

    flat = []
    for ui, (pre, pros, gen, start) in enumerate(plan):
        for pi_, pf in enumerate(pros):
            flat.append([pre if pi_ == 0 else None, list(pf), False])
    fstate = {"i": 0}

    def emit_next_prologue(nparts=None):
        fi = fstate["i"]
        if fi >= len(flat):
            if nparts is None and not fstate.get("out01"):
                fstate["out01"] = True
                flush_deferred()
                out_stage([0, 1])
            return
        ent = flat[fi]
        if not ent[2]:
            flush_deferred()
            if ent[0] is not None:
                ent[0]()
            ent[2] = True
        n = len(ent[1]) if nparts is None else min(nparts, len(ent[1]))
        for _ in range(n):
            ent[1].pop(0)()
        if nparts is None:
            fstate["i"] = fi + 1

    emit_next_prologue()
    for (pre, pros, gen, start) in plan:
        if start is not None:
            start()
        for y in gen():
            emit_next_prologue(y)
    flush_hooks()
    flush_deferred()
    gprev, bprev, gb_res = ln_affine(n_layers - 1, 2)

    out_stage([2, 3] if fstate.get("out01") else [0, 1, 2, 3])
    k.sp.h.wait_ge(s_out.sem, s_out.n)
    es.close()
    return nc


PK_OFF = {}
_o = 0
for _name, _n in (("c", NDT), ("b_ada", DEPTH * 72), ("ln", DEPTH * 3 * 2 * NDT), ("b_in_u", N_A * 16),
                  ("b_ada_kv", 16)):
    PK_OFF[_name] = (_o, _n)
    _o += _n
PK_COLS = _o


def _fm(v):
    v = np.asarray(v, np.float32)
    lead = v.shape[:-1]
    n = v.shape[-1] // 128
    a = v.reshape(lead + (n, 128))
    a = np.moveaxis(a, -1, 0)
    return np.ascontiguousarray(a.reshape(128, -1))


def make_in_maps(inputs):
    f = lambda a: np.ascontiguousarray(np.asarray(a, np.float32))
    x = f(inputs["x"])
    c = f(inputs["c"])
    b_ada = f(inputs["b_ada"])
    ln = np.stack([f(inputs["ln_g"]), f(inputs["ln_b"])], axis=2)
    b_in = f(inputs["gmlp_b_in"])
    rel = f(inputs["attn_rel_bias"])
    j = np.arange(5)[:, None, None]
    r = np.arange(128)[None, :, None]
    t = np.arange(128)[None, None, :]
    dist = 128 * (4 - j) + t - r
    idx = np.clip(dist, -(HD - 1), 4 * 64) + (HD - 1)
    biasT = np.ascontiguousarray(rel[:, :, idx])
    gm_rows = np.zeros((N_A, 5, GH), np.float32)
    gm_rows[:, 4] = 1.0
    gm_rows[:, 0] = b_in[:, GH:]
    gm_rows[:, 1] = f(inputs["gmlp_ln_g"])
    gm_rows[:, 2] = f(inputs["gmlp_ln_b"])
    gm_rows[:, 3, :1024] = f(inputs["gmlp_b_s"]).reshape(N_A, 1024)
    shared = {
        "w_ada": f(inputs["w_ada"]), "ffn_gu": f(inputs["ffn_gu"]), "ffn_down": f(inputs["ffn_down"]),
        "gmlp_w_in": f(inputs["gmlp_w_in"]), "gmlp_w_out": f(inputs["gmlp_w_out"]),
        "gmlp_w_sT": np.ascontiguousarray(f(inputs["gmlp_w_s"]).transpose(0, 3, 1, 2).reshape(N_A, 128, 1024)),
        "gm_rows": gm_rows,
        "w_ada_kv": f(inputs["w_ada_kv"]), "w_kv": f(inputs["w_kv"]),
        "attn_w_q": f(inputs["attn_w_q"]), "attn_w_o": f(inputs["attn_w_o"]), "biasT": biasT,
        "ident": np.eye(128, dtype=np.float32),
    }
    maps = []
    for b in range(NB):
        pkb = np.concatenate([
            _fm(c[b]), _fm(b_ada), _fm(ln), _fm(b_in[:, :GH]), _fm(f(inputs["b_ada_kv"]))], axis=1)
        assert pkb.shape == (128, PK_COLS), pkb.shape
        m = dict(shared)
        m["xT"] = np.ascontiguousarray(x[b].T)
        m["pk"] = np.ascontiguousarray(pkb)
        maps.append(m)
    return maps


_NC_CACHE = {}


def run(inputs, n_layers=DEPTH, dbg=None, trace=False):
    key = (n_layers, dbg)
    if key not in _NC_CACHE:
        _NC_CACHE[key] = build_program(n_layers, dbg)
    nc = _NC_CACHE[key]
    maps = make_in_maps(inputs)
    res = run_bass_kernel_spmd(nc, maps, core_ids=list(range(NB)), trace=trace)
    out = np.stack([np.ascontiguousarray(r["outT"].T) for r in res.results], axis=0)
    return out.astype(np.float32), res


def kernel(**inputs):
    out, _ = run(inputs)
    return out
```

```python
import collections
import contextlib
import numpy as np
import concourse.bass as bass
import concourse.mybir as mybir
from concourse.bass_utils import run_bass_kernel_spmd

F32 = mybir.dt.float32
BF16 = mybir.dt.bfloat16
AF = mybir.ActivationFunctionType
ALU = mybir.AluOpType

D = 1024
S = 2048
NB = 8
DEPTH = 4
N_A = 2
DFF = 2816
NFC = DFF // 128
GW = 4096
GH = 2048
NH = 16
HD = 64
N_REL = 320
ALPHA = (2.0 * DEPTH) ** 0.25
EPS = 1e-5
NDT = D // 128
TT = 512
SLOT = 5632
NSLOT = 3
NEG = -30000.0


class Src:
    __slots__ = ("sem", "n", "name")

    def __init__(self, sem, name):
        self.sem = sem
        self.n = 0
        self.name = name


class Eng(Src):
    __slots__ = ("h", "seen", "selfsync")

    def __init__(self, sem, name, h, selfsync):
        super().__init__(sem, name)
        self.h = h
        self.seen = {}
        self.selfsync = selfsync


class Res:
    __slots__ = ("w", "r")

    def __init__(self, fence=None):
        self.w = dict(fence) if fence else {}
        self.r = {}


class K:
    def __init__(self, nc, es):
        self.nc = nc
        self.es = es
        self.nsem = 0
        mk = lambda nm: es.enter_context(nc.semaphore(nm))
        self.pe = Eng(mk("s_pe"), "pe", nc.tensor, False)
        self.act = Eng(mk("s_act"), "act", nc.scalar, True)
        self.dve = Eng(mk("s_dve"), "dve", nc.vector, True)
        self.pool = Eng(mk("s_pool"), "pool", nc.gpsimd, True)
        self.sp = Eng(mk("s_sp"), "sp", nc.sync, False)
        self.engs = [self.pe, self.act, self.dve, self.pool, self.sp]
        self.dsrcs = []

    def dsrc(self, name):
        s = Src(self.es.enter_context(self.nc.semaphore(name)), name)
        self.dsrcs.append(s)
        return s

    def fence(self):
        f = {e: e.n for e in self.engs if e.n > 0}
        for s in self.dsrcs:
            if s.n > 0:
                f[s] = s.n
        return f

    def _waits(self, eng, reads, writes):
        deps = {}
        for r in reads:
            for s, c in r.w.items():
                if deps.get(s, 0) < c:
                    deps[s] = c
        for w in writes:
            for s, c in w.w.items():
                if deps.get(s, 0) < c:
                    deps[s] = c
            for s, c in w.r.items():
                if deps.get(s, 0) < c:
                    deps[s] = c
        for s, c in deps.items():
            if s is eng and not eng.selfsync:
                continue
            if eng.seen.get(s, 0) < c:
                eng.h.wait_ge(s.sem, c)
                eng.seen[s] = c

    def op(self, eng, fn, reads=(), writes=()):
        self._waits(eng, reads, writes)
        ins = fn()
        eng.n += 1
        ins.then_inc(eng.sem, 1)
        for r in reads:
            r.r[eng] = eng.n
        for w in writes:
            w.w = {eng: eng.n}
            w.r = {}
        return ins

    def dma(self, q, src, out, in_, reads=(), writes=(), **kw):
        self._waits(q, reads, writes)
        ins = q.h.dma_start(out=out, in_=in_, **kw)
        src.n += 16
        ins.then_inc(src.sem, 16)
        for r in reads:
            r.r[src] = src.n
        for w in writes:
            w.w = {src: src.n}
            w.r = {}
        return ins


def build_program(n_layers=DEPTH, dbg=None):
    nc = bass.Bass("TRN2", target_bir_lowering=False)
    es = contextlib.ExitStack()
    k = K(nc, es)

    def dram(name, shape, dt=F32, kind="ExternalInput"):
        return nc.dram_tensor(name, list(shape), dt, kind=kind).ap()

    xT = dram("xT", [D, S])
    pk = dram("pk", [128, PK_COLS])
    w_ada = dram("w_ada", [DEPTH, D, 9 * D])
    ffn_gu = dram("ffn_gu", [DEPTH, 2, D, 2 * DFF])
    ffn_down = dram("ffn_down", [DEPTH, 2, DFF, D])
    gm_w_in = dram("gmlp_w_in", [N_A, D, GW])
    gm_w_out = dram("gmlp_w_out", [N_A, GH, D])
    gm_w_sT = dram("gmlp_w_sT", [N_A, 128, 8 * 128])
    gm_rows = dram("gm_rows", [N_A, 5, GH])
    w_ada_kv = dram("w_ada_kv", [D, 2 * D])
    w_kv = dram("w_kv", [D, 2 * D])
    w_q = dram("attn_w_q", [2, D, D])
    w_o = dram("attn_w_o", [2, D, D])
    biasT = dram("biasT", [2, NH, 5, 128, 128])
    ident = dram("ident", [128, 128])
    outT = dram("outT", [D, S], kind="ExternalOutput")
    import os as _os
    _kvkind = "ExternalOutput" if _os.environ.get("KDBG_KV") else "Internal"
    KTd = dram("KT_scr", [D, S], BF16, kind=_kvkind)
    Vd = dram("V_scr", [S, D], BF16, kind=_kvkind)

    sb = lambda name, shape, dt=F32: es.enter_context(nc.sbuf_tensor(name, list(shape), dt))

    Nt = sb("N", [128, NDT, S])
    Nres = [[Res() for _ in range(S // TT)] for _ in range(NDT)]
    hT = sb("hT", [128, NDT, 2 * TT], BF16)
    hres = [[Res() for _ in range(2)] for _ in range(NDT)]
    pkt = sb("pkt", [128, PK_COLS])
    pk_res = Res()
    wslots = [sb(f"wslot{i}", [128, SLOT], BF16) for i in range(NSLOT)]
    wres = [Res() for _ in range(NSLOT)]
    wsrc = [k.dsrc(f"s_w{i}") for i in range(NSLOT)]
    wstate = {"i": 0}
    NTMP = 1
    sq = [sb(f"sq{i}", [128, TT]) for i in range(2)]
    sq_res = [Res() for _ in range(2)]
    s1 = [sb(f"s1_{i}", [128, TT]) for i in range(2)]
    s2 = [sb(f"s2_{i}", [128, TT]) for i in range(2)]
    st_res = [Res() for _ in range(2)]
    rstd_t = sb("rstd_t", [128, TT])
    stat_res = Res()
    ones32 = sb("ones32", [128, 128])
    onesb = sb("onesb", [128, 128], BF16)
    identb = sb("identb", [128, 128], BF16)
    const_res = Res()
    cact = sb("cact", [128, NDT], BF16)
    cact_res = Res()
    modT = sb("modT", [128, 72])
    mod_res = Res()
    coef = sb("coef", [128, 3 * DEPTH + 1, 5, NDT])
    coef_res = [Res() for _ in range(3 * DEPTH + 1)]
    ctmp = sb("ctmp", [128, NDT])
    ctmp_res = Res()

    psum = [es.enter_context(nc.psum_tensor(f"ps{i}", [128, TT], F32)) for i in range(8)]
    pres = [Res() for _ in range(8)]
    pstate = {"i": 0}

    deferred = collections.deque()
    dstate = {"in": False}

    def bank():
        if not dstate["in"] and deferred and not dstate.get("pause"):
            dstate["in"] = True
            deferred.popleft()()
            dstate["in"] = False
        i = pstate["i"]
        pstate["i"] = (i + 1) % 7
        return psum[i], pres[i]

    def flush_deferred():
        dstate["in"] = True
        while deferred:
            deferred.popleft()()
        dstate["in"] = False

    mod_pb, mod_pr = psum[7], pres[7]
    hooks = collections.deque()
    hstate = {"in": False, "cnt": 0}

    mb_state = {"buf": None, "res": None}
    mb_src = k.dsrc("s_modbuf")

    def run_hook():
        if hstate["in"] or not hooks or mb_state["buf"] is None:
            return
        hstate["in"] = True
        hooks.popleft()()
        hstate["in"] = False

    def flush_hooks():
        hstate["in"] = True
        while hooks:
            hooks.popleft()()
        hstate["in"] = False

    def wload(view_elems, src_aps, rearr=None):
        i = wstate["i"]
        wstate["i"] = (i + 1) % NSLOT
        slot = wslots[i]
        a, b = view_elems
        view = slot[:, 0:a * b].rearrange("p (a b) -> p a b", a=a)
        for (a0, a1, b0, b1, ap) in src_aps:
            k.dma(k.pool, wsrc[i], view[:, a0:a1, b0:b1], ap, writes=[wres[i]], max_dma_last_dim=4096)
        run_hook()
        return view, wres[i]

    def pkcol(name, j=None):
        o, n = PK_OFF[name]
        if j is None:
            return pkt[:, o:o + n]
        return pkt[:, o + j:o + j + 1]

    s_in = k.dsrc("s_in")
    k.dma(k.sp, s_in, pkt[:], pk[:], writes=[pk_res])
    xv = xT.rearrange("(dt p) t -> p dt t", p=128)
    s_x = [k.dsrc(f"s_x{i}") for i in range(4)]
    def load_x(tts):
        for tt in tts:
            k.dma(k.sp, s_x[tt], Nt[:, :, tt * TT:(tt + 1) * TT], xv[:, :, tt * TT:(tt + 1) * TT],
                  writes=[Nres[dt][tt] for dt in range(NDT)])

    load_x([0, 1])
    k.op(k.dve, lambda: nc.vector.memset(ones32[:], 1.0), writes=[const_res])
    k.op(k.dve, lambda: nc.vector.memset(onesb[:], 1.0), writes=[const_res])
    ident_res = Res()
    k.dma(k.pool, k.dsrc("s_ident"), identb[:], ident[:], writes=[ident_res])
    k.op(k.act, lambda: nc.scalar.activation(out=cact[:], in_=pkcol("c"), func=AF.Silu),
         reads=[pk_res], writes=[cact_res])

    gb0 = sb("gb0", [128, 2, NDT])
    gb0_res = Res()
    k.op(k.dve, lambda: nc.vector.memset(gb0[:, 0, :], 1.0), writes=[gb0_res])
    k.op(k.dve, lambda: nc.vector.memset(gb0[:, 1, :], 0.0), writes=[gb0_res])

    MC = 512
    modkv = sb("modkv", [128, 16])
    modkv_res = Res()

    def mod_chunks(wsrc_ap, ncols, bias_ap, dst, dst_res, pcol0, split=None):
        wv = wsrc_ap.rearrange("(kt p) n -> p kt n", p=128)
        out = []

        def chunk(c0):
            if mb_state["buf"] is not None:
                view, vres = mb_state["buf"], mb_state["res"]
                k.dma(k.pool, mb_src, view[:, :, :], wv[:, :, c0:c0 + MC], writes=[vres], max_dma_last_dim=4096)
            else:
                view, vres = wload((NDT, MC), [(0, NDT, 0, MC, wv[:, :, c0:c0 + MC])])
            for jj in range(MC // 128):
                j = c0 // 128 + jj
                for kt in range(NDT):
                    k.op(k.pe, lambda j=j, kt=kt, jj=jj: nc.tensor.matmul(
                        mod_pb[:, pcol0 + j:pcol0 + j + 1], view[:, kt, jj * 128:(jj + 1) * 128], cact[:, kt:kt + 1],
                        start=(kt == 0), stop=(kt == NDT - 1), skip_group_check=True),
                        reads=[vres, cact_res], writes=[mod_pr])

        def final(j0, j1):
            k.op(k.dve, lambda: nc.vector.tensor_tensor(out=dst[:, j0:j1], in0=mod_pb[:, pcol0 + j0:pcol0 + j1],
                                                        in1=bias_ap[:, j0:j1], op=ALU.add),
                 reads=[mod_pr, pk_res], writes=[dst_res])

        nchunk = ncols // MC
        jpc = MC // 128
        for ic in range(nchunk):
            out.append(lambda c0=ic * MC: chunk(c0))
            if split is not None and ic == split - 1:
                out.append(lambda: final(0, split * jpc))
        out.append(lambda: final(0 if split is None else split * jpc, ncols // 128))
        return out

    def make_coef(ci, gprev, bprev, gb_res, ish, isc, igate, wgt, msrc=None, msrc_res=None):
        cr = coef_res[ci]
        c = lambda q: coef[:, ci, q, :]
        msrc = modT if msrc is None else msrc
        mod_res_ = mod_res if msrc_res is None else msrc_res
        md = lambda i: msrc[:, i * NDT:(i + 1) * NDT]
        V = nc.vector
        k.op(k.dve, lambda: V.tensor_scalar(out=ctmp[:], in0=md(isc), scalar1=1.0, scalar2=None, op0=ALU.add),
             reads=[mod_res_], writes=[ctmp_res])
        k.op(k.dve, lambda: V.tensor_tensor(out=c(0), in0=gprev, in1=ctmp[:], op=ALU.mult),
             reads=[ctmp_res, gb_res], writes=[cr])
        k.op(k.dve, lambda: V.tensor_tensor(out=c(1), in0=bprev, in1=ctmp[:], op=ALU.mult),
             reads=[ctmp_res, gb_res], writes=[cr])
        k.op(k.dve, lambda: V.tensor_tensor(out=c(1), in0=c(1), in1=md(ish), op=ALU.add),
             reads=[mod_res_, cr], writes=[cr])
        if igate is not None:
            k.op(k.dve, lambda: V.tensor_scalar(out=c(2), in0=gprev, scalar1=ALPHA, scalar2=None, op0=ALU.mult),
                 reads=[gb_res, cr], writes=[cr])
            k.op(k.dve, lambda: V.tensor_scalar(out=c(3), in0=bprev, scalar1=ALPHA, scalar2=None, op0=ALU.mult),
                 reads=[gb_res, cr], writes=[cr])
            k.op(k.dve, lambda: V.tensor_scalar(out=c(4), in0=md(igate), scalar1=wgt, scalar2=wgt,
                                                op0=ALU.mult, op1=ALU.add),
                 reads=[mod_res_, cr], writes=[cr])

    def modulate(ci, tts, hoff=0):
        for li, gt in enumerate(tts):
            for dt in range(NDT):
                k.op(k.act, lambda dt=dt, li=li, gt=gt: nc.scalar.activation(
                    out=hT[:, dt, (hoff + li) * TT:(hoff + li + 1) * TT], in_=Nt[:, dt, gt * TT:(gt + 1) * TT],
                    func=AF.Identity, scale=coef[:, ci, 0, dt:dt + 1], bias=coef[:, ci, 1, dt:dt + 1]),
                    reads=[Nres[dt][gt], coef_res[ci]], writes=[hres[dt][hoff + li]])

    def prescale(ci, tts):
        for gt in tts:
            for dt in range(NDT):
                k.op(k.act, lambda dt=dt, gt=gt: nc.scalar.activation(
                    out=Nt[:, dt, gt * TT:(gt + 1) * TT], in_=Nt[:, dt, gt * TT:(gt + 1) * TT],
                    func=AF.Identity, scale=coef[:, ci, 2, dt:dt + 1], bias=coef[:, ci, 3, dt:dt + 1]),
                    reads=[coef_res[ci]], writes=[Nres[dt][gt]])

    def epilogue(ci, dt, gt, pb, pr, si):
        nsl = Nt[:, dt, gt * TT:(gt + 1) * TT]
        V = nc.vector
        k.op(k.dve, lambda: V.scalar_tensor_tensor(out=nsl, in0=pb[:], scalar=coef[:, ci, 4, dt:dt + 1], in1=nsl,
                                                   op0=ALU.mult, op1=ALU.add),
             reads=[pr, coef_res[ci]], writes=[Nres[dt][gt]])
        if dt == 0:
            k.op(k.act, lambda: nc.scalar.activation(out=s2[si][:], in_=nsl, func=AF.Square),
                 reads=[Nres[dt][gt]], writes=[st_res[si]])
            k.op(k.dve, lambda: V.tensor_copy(out=s1[si][:], in_=nsl), reads=[Nres[dt][gt]], writes=[st_res[si]])
        else:
            qi = dt % 2
            k.op(k.act, lambda: nc.scalar.activation(out=sq[qi][:], in_=nsl, func=AF.Square),
                 reads=[Nres[dt][gt]], writes=[sq_res[qi]])
            k.op(k.dve, lambda: V.tensor_tensor(out=s1[si][:], in0=s1[si][:], in1=nsl, op=ALU.add),
                 reads=[Nres[dt][gt]], writes=[st_res[si]])
            k.op(k.dve, lambda: V.tensor_tensor(out=s2[si][:], in0=s2[si][:], in1=sq[qi][:], op=ALU.add),
                 reads=[sq_res[qi]], writes=[st_res[si]])

    def ln_finish(gt, si):
        V = nc.vector

        mean_t, nmr_t = s1[si], s2[si]

        def c1():
            p1, r1 = bank()
            p2, r2 = bank()
            k.op(k.pe, lambda: nc.tensor.matmul(p1[:], ones32[:], s1[si][:], start=True, stop=True),
                 reads=[st_res[si], const_res], writes=[r1])
            k.op(k.pe, lambda: nc.tensor.matmul(p2[:], ones32[:], s2[si][:], start=True, stop=True),
                 reads=[st_res[si], const_res], writes=[r2])
            k.op(k.dve, lambda: V.tensor_scalar(out=mean_t[:], in0=p1[:], scalar1=1.0 / D, scalar2=None, op0=ALU.mult),
                 reads=[r1], writes=[st_res[si]])
            k.op(k.act, lambda: nc.scalar.activation(out=nmr_t[:], in_=mean_t[:], func=AF.Square),
                 reads=[st_res[si]], writes=[st_res[si]])
            k.op(k.dve, lambda: V.scalar_tensor_tensor(out=rstd_t[:], in0=p2[:], scalar=1.0 / D, in1=nmr_t[:],
                                                       op0=ALU.mult, op1=ALU.subtract),
                 reads=[r2, st_res[si]], writes=[stat_res])

        def c2():
            pass

        def c3():
            k.op(k.dve, lambda: V.tensor_scalar(out=rstd_t[:], in0=rstd_t[:], scalar1=EPS, scalar2=None, op0=ALU.add),
                 reads=[stat_res], writes=[stat_res])
            k.op(k.act, lambda: nc.scalar.activation(out=rstd_t[:], in_=rstd_t[:], func=AF.Sqrt),
                 reads=[stat_res], writes=[stat_res])

        def c4():
            k.op(k.dve, lambda: V.reciprocal(out=rstd_t[:], in_=rstd_t[:]),
                 reads=[stat_res], writes=[stat_res])
            k.op(k.dve, lambda: V.scalar_tensor_tensor(out=nmr_t[:], in0=mean_t[:], scalar=-1.0, in1=rstd_t[:],
                                                       op0=ALU.mult, op1=ALU.mult),
                 reads=[stat_res, st_res[si]], writes=[st_res[si]])

        def cn(dt):
            nsl = Nt[:, dt, gt * TT:(gt + 1) * TT]
            k.op(k.dve, lambda: V.tensor_tensor(out=nsl, in0=nsl, in1=rstd_t[:], op=ALU.mult),
                 reads=[stat_res], writes=[Nres[dt][gt]])
            k.op(k.dve, lambda: V.tensor_tensor(out=nsl, in0=nsl, in1=nmr_t[:], op=ALU.add),
                 reads=[st_res[si]], writes=[Nres[dt][gt]])

        deferred.extend([c1, c2, c3, c4] + [(lambda dt=dt: cn(dt)) for dt in range(NDT)])

    def ffn(ci, l, which):
        wg = ffn_gu[l, which].rearrange("(kt p) n -> p kt n", p=128)
        wd = ffn_down[l, which].rearrange("(fc p) n -> p fc n", p=128)
        fen = k.fence()
        with nc.sbuf_tensor(f"hid{ci}", [128, NFC, 2 * TT], BF16) as hid, \
                nc.sbuf_tensor(f"sg{ci}", [128, NTMP, TT], F32) as sgt, \
                nc.sbuf_tensor(f"modbuf{ci}", [128, NDT, MC], BF16) as mbt:
            mb_state["buf"], mb_state["res"] = mbt, Res(fen)
            hid_res = [[Res(fen) for _ in range(2)] for _ in range(NFC)]
            sg = [sgt[:, i, :] for i in range(NTMP)]
            sg_res = [Res(fen) for _ in range(NTMP)]
            for half in range(2):
                tts = [2 * half, 2 * half + 1]
                for f0 in range(0, NFC, 2):
                    view, vres = wload((NDT, 512), [
                        (0, NDT, 0, 256, wg[:, :, f0 * 128:f0 * 128 + 256]),
                        (0, NDT, 256, 512, wg[:, :, DFF + f0 * 128:DFF + f0 * 128 + 256])])
                    for fi in range(2):
                        fc = f0 + fi
                        bg = [bank() for _ in range(2)]
                        bu = [bank() for _ in range(2)]
                        for (bks, coff) in ((bg, fi * 128), (bu, 256 + fi * 128)):
                            for kt in range(NDT):
                                for li in range(2):
                                    pb, pr = bks[li]
                                    k.op(k.pe, lambda pb=pb, kt=kt, li=li, coff=coff: nc.tensor.matmul(
                                        pb[:], view[:, kt, coff:coff + 128], hT[:, kt, li * TT:(li + 1) * TT],
                                        start=(kt == 0), stop=(kt == NDT - 1)),
                                        reads=[vres, hres[kt][li]], writes=[pr])
                        for li in range(2):
                            ti = (fc * 2 + li) % NTMP
                            pg, rg = bg[li]
                            pu, ru = bu[li]
                            k.op(k.act, lambda pg=pg, ti=ti: nc.scalar.activation(out=sg[ti], in_=pg[:], func=AF.Silu),
                                 reads=[rg], writes=[sg_res[ti]])
                            k.op(k.dve, lambda pu=pu, ti=ti, fc=fc, li=li: nc.vector.tensor_tensor(
                                out=hid[:, fc, li * TT:(li + 1) * TT], in0=pu[:], in1=sg[ti], op=ALU.mult),
                                reads=[ru, sg_res[ti]], writes=[hid_res[fc][li]])
                yield
                for d0 in range(0, NDT, 2):
                    view, vres = wload((NFC, 256), [(0, NFC, 0, 256, wd[:, :, d0 * 128:d0 * 128 + 256])])
                    for di in range(2):
                        dt = d0 + di
                        for li in range(2):
                            pb, pr = bank()
                            for fc in range(NFC):
                                k.op(k.pe, lambda pb=pb, fc=fc, li=li, di=di: nc.tensor.matmul(
                                    pb[:], view[:, fc, di * 128:(di + 1) * 128], hid[:, fc, li * TT:(li + 1) * TT],
                                    start=(fc == 0), stop=(fc == NFC - 1)),
                                    reads=[vres, hid_res[fc][li]], writes=[pr])
                            epilogue(ci, dt, tts[li], pb, pr, li)
                for li in range(2):
                    ln_finish(tts[li], li)
            flush_hooks()
            mb_state["buf"], mb_state["res"] = None, None

    def gmlp(ci, l):
        V = nc.vector
        w_in = gm_w_in[l].rearrange("(kt p) n -> p kt n", p=128)
        w_out = gm_w_out[l].rearrange("(cc p) n -> p cc n", p=128)
        fen = k.fence()
        with contextlib.ExitStack() as gs:
            gsb = lambda name, shape, dt=F32: gs.enter_context(nc.sbuf_tensor(f"{name}_{l}", list(shape), dt))
            u = gsb("g_u", [128, 16, TT], BF16)
            u_res = [Res(fen) for _ in range(16)]
            vpre = gsb("g_vpre", [128, 4, GH])
            vpre_res = [[Res(fen) for _ in range(4)] for _ in range(4)]
            vb = gsb("g_vb", [128, 2, GH], BF16)
            vb_res = [Res(fen) for _ in range(2)]
            gbc = gsb("g_gbc", [128, GH])
            wsT = gsb("g_wsT", [128, 1024], BF16)
            Lr = gsb("g_L", [128, GH], BF16)
            Rr = gsb("g_R", [128, 1024], BF16)
            smal = gsb("g_small", [128, 4, 4 * 6 + 2 + 2])
            smal_res = [Res(fen) for _ in range(4)]
            gbc_res, ws_res, L_res, R_res, r64_res, rtmp_res = (Res(fen) for _ in range(6))
            sg_ = [k.dsrc(f"s_gm{l}_{i}") for i in range(9)]
            k.dma(k.sp, sg_[0], gbc[:], gm_rows[l, 1:2, :].partition_broadcast(128), writes=[gbc_res])
            k.dma(k.sp, sg_[1], sq[0][32:33, :], gm_rows[l, 3:4, 0:512], writes=[sq_res[0]])
            k.dma(k.sp, sg_[5 + 3 - 1], sq[1][32:33, :], gm_rows[l, 3:4, 512:1024], writes=[sq_res[1]])
            k.dma(k.pool, sg_[2], Lr[0:1, :], gm_rows[l, 2:3, :], writes=[L_res], max_dma_last_dim=4096)
            k.dma(k.pool, sg_[3], Lr[64:65, :], gm_rows[l, 0:1, :], writes=[r64_res], max_dma_last_dim=4096)
            k.dma(k.pool, sg_[4], wsT[:], gm_w_sT[l], writes=[ws_res], max_dma_last_dim=4096)
            L12_res = [Res(fen), Res(fen)]
            sg_ones = [k.dsrc(f"s_gm{l}_ones{i}") for i in range(2)]
            for rr in (1, 2):
                k.dma(k.pool, sg_ones[rr - 1], Lr[rr:rr + 1, :], gm_rows[l, 4:5, :], writes=[L12_res[rr - 1]],
                      max_dma_last_dim=4096)
            wsT3 = wsT[:].rearrange("p (g t) -> p g t", g=8)
            k.op(k.dve, lambda: V.memset(wsT3[64:128, :, 0:64], 0.0), writes=[ws_res])
            for hh in range(2):
                pb, pr = bank()
                k.op(k.pe, lambda pb=pb, hh=hh: nc.tensor.matmul(pb[0:1, :], onesb[:, 0:1], wsT[:, hh * 512:(hh + 1) * 512],
                                                                start=True, stop=True),
                     reads=[ws_res, const_res], writes=[pr])
                k.op(k.dve, lambda pb=pb, hh=hh: V.tensor_copy(out=Rr[0:1, hh * 512:(hh + 1) * 512], in_=pb[0:1, :]),
                     reads=[pr], writes=[R_res])
            t32 = Rr[32:33, :]
            for hh, src_t in ((0, sq[0]), (1, sq[1])):
                k.op(k.dve, lambda hh=hh, src_t=src_t: V.tensor_copy(out=Rr[32:33, hh * 512:(hh + 1) * 512], in_=src_t[32:33, :]),
                     reads=[sq_res[0], sq_res[1]], writes=[rtmp_res])
            k.dma(k.sp, sg_[6], Rr[1:2, :], t32, reads=[rtmp_res], writes=[R_res])
            for hh, src_t in ((0, sq[0]), (1, sq[1])):
                k.op(k.dve, lambda hh=hh, src_t=src_t: V.tensor_tensor(
                    out=Rr[32:33, hh * 512:(hh + 1) * 512], in0=src_t[32:33, :], in1=Rr[32:33, hh * 512:(hh + 1) * 512],
                    op=ALU.subtract),
                    reads=[sq_res[0], sq_res[1]], writes=[rtmp_res])
            k.dma(k.sp, sg_[8], Rr[2:3, :], t32, reads=[rtmp_res], writes=[R_res])

            for p in range(S // TT):
                gt = p
                ho = p % 2
                si = p % 2
                hsl = lambda kt, a, b, ho=ho: hT[:, kt, ho * TT + a:ho * TT + b]

                def uphase():
                    for c0 in range(0, 16, 4):
                        view, vres = wload((NDT, 512), [(0, NDT, 0, 512, w_in[:, :, c0 * 128:c0 * 128 + 512])])
                        for cj in range(4):
                            cc = c0 + cj
                            pb, pr = bank()
                            for kt in range(NDT):
                                k.op(k.pe, lambda pb=pb, kt=kt, cj=cj: nc.tensor.matmul(
                                    pb[:], view[:, kt, cj * 128:(cj + 1) * 128], hsl(kt, 0, TT),
                                    start=(kt == 0), stop=(kt == NDT - 1)),
                                    reads=[vres, hres[kt][ho]], writes=[pr])
                            o, _ = PK_OFF["b_in_u"]
                            bcol = pkt[:, o + l * 16 + cc:o + l * 16 + cc + 1]
                            k.op(k.act, lambda pb=pb, cc=cc, bcol=bcol: nc.scalar.activation(
                                out=u[:, cc, :], in_=pb[:], func=AF.Gelu, bias=bcol),
                                reads=[pr, pk_res], writes=[u_res[cc]])

                def vphase():
                    for vc in range(4):
                        view, vres = wload((NDT, 512), [(0, NDT, 0, 512, w_in[:, :, GH + vc * 512:GH + (vc + 1) * 512])])
                        for w in range(4):
                            pb, pr = bank()
                            for kt in range(NDT):
                                k.op(k.pe, lambda pb=pb, kt=kt, w=w: nc.tensor.matmul(
                                    pb[:], hsl(kt, w * 128, (w + 1) * 128), view[:, kt, :],
                                    start=(kt == 0), stop=False),
                                    reads=[vres, hres[kt][ho]], writes=[pr])
                            k.op(k.pe, lambda pb=pb, vc=vc: nc.tensor.matmul(
                                pb[:], onesb[64:65, :], Lr[64:65, vc * 512:(vc + 1) * 512], start=False, stop=True),
                                reads=[r64_res, const_res], writes=[pr])
                            k.op(k.act, lambda pb=pb, w=w, vc=vc: nc.scalar.activation(
                                out=vpre[:, w, vc * 512:(vc + 1) * 512], in_=pb[:], func=AF.Gelu),
                                reads=[pr], writes=[vpre_res[w][vc]])

                def stats_all():
                    for w in range(4):
                        sm = smal[:, w, :]
                        for vc in range(4):
                            k.op(k.dve, lambda vc=vc, w=w, sm=sm: V.bn_stats(
                                out=sm[:, vc * 6:(vc + 1) * 6], in_=vpre[:, w, vc * 512:(vc + 1) * 512]),
                                reads=[vpre_res[w][vc]], writes=[smal_res[0]])
                        k.op(k.dve, lambda sm=sm: V.bn_aggr(out=sm[:, 24:26], in_=sm[:, 0:24]),
                             reads=[smal_res[0]], writes=[smal_res[0]])
                    var4, rs4, mu4, nb4 = smal[:, :, 25], smal[:, :, 26], smal[:, :, 24], smal[:, :, 27]
                    k.op(k.dve, lambda: V.tensor_scalar(out=rs4, in0=var4, scalar1=EPS, scalar2=None, op0=ALU.add),
                         reads=[smal_res[0]], writes=[smal_res[0]])
                    k.op(k.act, lambda: nc.scalar.activation(out=rs4, in_=rs4, func=AF.Sqrt),
                         reads=[smal_res[0]], writes=[smal_res[0]])
                    k.op(k.dve, lambda: V.reciprocal(out=rs4, in_=rs4), reads=[smal_res[0]], writes=[smal_res[0]])
                    k.op(k.dve, lambda: V.scalar_tensor_tensor(out=nb4, in0=mu4, scalar=-1.0, in1=rs4,
                                                               op0=ALU.mult, op1=ALU.mult),
                         reads=[smal_res[0]], writes=[smal_res[0]])

                def norm(w):
                    sm = smal[:, w, :]
                    k.op(k.act, lambda: nc.scalar.activation(
                        out=vpre[:, w, :], in_=vpre[:, w, :], func=AF.Identity, scale=sm[:, 26:27], bias=sm[:, 27:28]),
                        reads=[smal_res[0]], writes=vpre_res[w])

                def vbmul(w):
                    wi = w % 2
                    k.op(k.dve, lambda: V.tensor_tensor(out=vb[:, wi, :], in0=vpre[:, w, :], in1=gbc[:], op=ALU.mult),
                         reads=vpre_res[w] + [gbc_res], writes=[vb_res[wi]])

                def spatial(wi, w):
                    for cg in range(4):
                        pb, pr = bank()
                        for cj in range(4):
                            cc = cg * 4 + cj
                            g = cc // 2
                            k.op(k.pe, lambda pb=pb, cj=cj, cc=cc, g=g: nc.tensor.matmul(
                                pb[:, cj * 128:(cj + 1) * 128], vb[:, wi, cc * 128:(cc + 1) * 128],
                                wsT[:, g * 128:(g + 1) * 128], start=(cj == 0), stop=False, skip_group_check=True),
                                reads=[vb_res[wi], ws_res], writes=[pr])
                        for cj in range(4):
                            cc = cg * 4 + cj
                            g = cc // 2
                            k.op(k.pe, lambda pb=pb, cj=cj, cc=cc, g=g: nc.tensor.matmul(
                                pb[:, cj * 128:(cj + 1) * 128], Lr[0:3, cc * 128:(cc + 1) * 128],
                                Rr[0:3, g * 128:(g + 1) * 128], start=False, stop=True, skip_group_check=True),
                                reads=[L_res, R_res] + L12_res, writes=[pr])
                        usl = u[:, cg * 4:(cg + 1) * 4, w * 128:(w + 1) * 128]
                        k.op(k.dve, lambda pb=pb, usl=usl: V.tensor_tensor(
                            out=usl, in0=pb[:].rearrange("p (a b) -> p a b", a=4), in1=usl, op=ALU.mult),
                            reads=[pr], writes=u_res[cg * 4:(cg + 1) * 4])

                dstate["pause"] = True
                vphase()
                stats_all()
                norm(0)
                vbmul(0)
                norm(1)
                vbmul(1)
                norm(2)
                norm(3)
                dstate["pause"] = False
                uphase()
                spatial(0, 0)
                vbmul(2)
                spatial(1, 1)
                vbmul(3)
                spatial(0, 2)
                spatial(1, 3)
                yield
                for d0 in range(0, NDT, 2):
                    view, vres = wload((16, 256), [(0, 16, 0, 256, w_out[:, :, d0 * 128:d0 * 128 + 256])])
                    for di in range(2):
                        dt = d0 + di
                        pb, pr = bank()
                        for cc in range(16):
                            k.op(k.pe, lambda pb=pb, cc=cc, di=di: nc.tensor.matmul(
                                pb[:], view[:, cc, di * 128:(di + 1) * 128], u[:, cc, :],
                                start=(cc == 0), stop=(cc == 15)),
                                reads=[vres, u_res[cc]], writes=[pr])
                        epilogue(ci, dt, gt, pb, pr, si)
                ln_finish(gt, si)

    kv_res = Res()
    s_kvst = [k.dsrc(f"s_kvst{i}") for i in range(4)]

    def kvload(view_elems, out_in_pairs):
        i = wstate["i"]
        wstate["i"] = (i + 1) % NSLOT
        a, b = view_elems
        view = wslots[i][:, 0:a * b].rearrange("p (a b) -> p a b", a=a)
        for (sl, ap) in out_in_pairs:
            k.dma(k.pool, wsrc[i], sl(view), ap, reads=[kv_res], writes=[wres[i]])
        return view, wres[i]

    def kv_produce(ci):
        KTv = KTd.rearrange("(dt p) s -> p dt s", p=128)
        wkv = w_kv.rearrange("(kt p) n -> p kt n", p=128)
        fen = k.fence()
        with nc.sbuf_tensor("kvst", [128, 4, TT], BF16) as kvst:
            st_r = [Res(fen) for _ in range(4)]
            cnt = [0]

            def evac_store(pb, pr, dst):
                i = cnt[0] % 4
                cnt[0] += 1
                if i % 2 == 0:
                    k.op(k.act, lambda: nc.scalar.copy(out=kvst[:, i, :], in_=pb[:]), reads=[pr], writes=[st_r[i]])
                else:
                    k.op(k.dve, lambda: nc.vector.tensor_copy(out=kvst[:, i, :], in_=pb[:]), reads=[pr], writes=[st_r[i]])
                k.dma(k.sp, s_kvst[i], dst, kvst[:, i, :], reads=[st_r[i]])

            for half in range(2):
                tts = [2 * half, 2 * half + 1]
                for c0 in (0, 512):
                    view, vres = wload((NDT, 512), [(0, NDT, 0, 512, wkv[:, :, c0:c0 + 512])])
                    for dj in range(4):
                        dt = c0 // 128 + dj
                        for li in range(2):
                            pb, pr = bank()
                            for kt in range(NDT):
                                k.op(k.pe, lambda pb=pb, kt=kt, dj=dj, li=li: nc.tensor.matmul(
                                    pb[:], view[:, kt, dj * 128:(dj + 1) * 128], hT[:, kt, li * TT:(li + 1) * TT],
                                    start=(kt == 0), stop=(kt == NDT - 1)),
                                    reads=[vres, hres[kt][li]], writes=[pr])
                            evac_store(pb, pr, KTv[:, dt, tts[li] * TT:(tts[li] + 1) * TT])
                for c0 in (0, 512):
                    view, vres = wload((NDT, 512), [(0, NDT, 0, 512, wkv[:, :, D + c0:D + c0 + 512])])
                    for tb in range(8):
                        li = tb // 4
                        tok0 = half * 1024 + tb * 128
                        pb, pr = bank()
                        for kt in range(NDT):
                            k.op(k.pe, lambda pb=pb, kt=kt, tb=tb: nc.tensor.matmul(
                                pb[:], hT[:, kt, tb * 128:(tb + 1) * 128], view[:, kt, :],
                                start=(kt == 0), stop=(kt == NDT - 1)),
                                reads=[vres, hres[kt][li]], writes=[pr])
                        evac_store(pb, pr, Vd[tok0:tok0 + 128, c0:c0 + 512])
                if half == 1:
                    kv_res.w = {sx: sx.n for sx in s_kvst}
                yield

    bias_state = {"buf": None, "res": Res()}

    def bias_hooks(l):
        j_l = l - N_A
        if bias_state["buf"] is None:
            bias_state["buf"] = sb("a_bias", [128, NH * 5, 128], BF16)
            bias_state["res"] = Res(k.fence())
        biasb, bias_res = bias_state["buf"], bias_state["res"]
        bsrc = biasT[j_l].rearrange("h j r t -> r (h j) t")
        s_b = [k.dsrc(f"s_bias{l}_{i}") for i in range(8)]
        piece_res = [Res() for _ in range(8)]
        b4 = biasb[:].rearrange("p (h j) t -> p h j t", j=5)
        out = []

        def piece(i):
            h0 = 2 * i
            k.dma(k.pool, s_b[i], biasb[:, h0 * 5:(h0 + 2) * 5, :], bsrc[:, h0 * 5:(h0 + 2) * 5, :],
                  reads=[], writes=[piece_res[i], bias_res] if i == 0 else [piece_res[i]])

        def masks():
            k.op(k.dve, lambda: nc.vector.memset(b4[64:128, :, 4, 0:64], NEG), reads=piece_res, writes=[bias_res])
            k.op(k.dve, lambda: nc.vector.memset(b4[0:64, :, 0, 64:128], NEG), writes=[bias_res])

        for i in range(8):
            out.append(lambda i=i: piece(i))
        out.append(masks)
        return out

    def attention(ci, l):
        V = nc.vector
        j_l = l - N_A
        wq = w_q[j_l].rearrange("(kt p) n -> p kt n", p=128)
        wo = w_o[j_l].rearrange("(kt p) n -> p kt n", p=128)
        KTv = KTd.rearrange("(dt p) s -> p dt s", p=128)
        Vv = Vd.rearrange("(kt p) c -> p kt c", p=128)
        fen = k.fence()
        with contextlib.ExitStack() as gs:
            gsb = lambda name, shape, dt=F32: gs.enter_context(nc.sbuf_tensor(f"{name}_{l}", list(shape), dt))
            qT = gsb("a_qT", [128, NDT, TT], BF16)
            q_res = [Res(fen) for _ in range(NDT)]
            oT = gsb("a_oT", [128, NDT, TT], BF16)
            o_res = [Res(fen) for _ in range(NDT)]
            biasb, bias_res = bias_state["buf"], bias_state["res"]
            NPT = 2
            PT = [gsb(f"a_PT{i}", [128, 5, 1024], BF16) for i in range(NPT)]
            pt_res = [[Res(fen) for _ in range(5)] for _ in range(NPT)]
            rinv = [gsb(f"a_rinv{i}", [128, TT]) for i in range(2)]
            rinv_res = [Res(fen) for _ in range(2)]
            for qp in range(S // TT):
                gt = qp
                ho = qp % 2
                si = qp % 2
                for c0 in (0, 512):
                    view, vres = wload((NDT, 512), [(0, NDT, 0, 512, wq[:, :, c0:c0 + 512])])
                    for dj in range(4):
                        dt = c0 // 128 + dj
                        pb, pr = bank()
                        for kt in range(NDT):
                            k.op(k.pe, lambda pb=pb, kt=kt, dj=dj: nc.tensor.matmul(
                                pb[:], view[:, kt, dj * 128:(dj + 1) * 128], hT[:, kt, ho * TT:(ho + 1) * TT],
                                start=(kt == 0), stop=(kt == NDT - 1)),
                                reads=[vres, hres[kt][ho]], writes=[pr])
                        k.op(k.act, lambda pb=pb, dt=dt: nc.scalar.activation(
                            out=qT[:, dt, :], in_=pb[:], func=AF.Copy, scale=HD ** -0.5),
                            reads=[pr], writes=[q_res[dt]])
                band = {}

                def qk(ku):
                    m, hg = ku // 2, ku % 2
                    pm = 4 * qp + m
                    nj = min(5, pm + 1)
                    j0 = 5 - nj
                    kt0 = pm - 4 + j0
                    if hg == 0:
                        Kb, kres = kvload((NDT, nj * 128), [(lambda v: v[:, :, :], KTv[:, :, kt0 * 128:(pm + 1) * 128])])
                        Vb, vres_ = kvload((nj, D), [(lambda v: v[:, :, :], Vv[:, kt0:pm + 1, :])])
                        band[m] = (Kb, kres, Vb, vres_)
                    Kb, kres, Vb, vres_ = band[m]
                    qs = slice(m * 128, (m + 1) * 128)
                    pi = ku % NPT
                    P = PT[pi]
                    for jj in range(nj):
                        j = j0 + jj
                        bA = bank()
                        bB = bank()
                        for di in range(4):
                            dt = hg * 4 + di
                            for hh, (pb, pr) in ((0, bA), (1, bB)):
                                k.op(k.pe, lambda pb=pb, dt=dt, hh=hh, di=di, jj=jj: nc.tensor.matmul(
                                    pb[:, di * 128:(di + 1) * 128],
                                    Kb[hh * 64:(hh + 1) * 64, dt, jj * 128:(jj + 1) * 128],
                                    qT[hh * 64:(hh + 1) * 64, dt, qs], start=(di == 0), stop=False,
                                    skip_group_check=True),
                                    reads=[kres, q_res[dt]], writes=[pr])
                        for hh, (pb, pr) in ((0, bA), (1, bB)):
                            for di in range(4):
                                h = (hg * 4 + di) * 2 + hh
                                k.op(k.pe, lambda pb=pb, di=di, h=h, j=j: nc.tensor.matmul(
                                    pb[:, di * 128:(di + 1) * 128], identb[:], biasb[:, h * 5 + j, :],
                                    start=False, stop=True, skip_group_check=True),
                                    reads=[bias_res, ident_res], writes=[pr])
                            k.op(k.act, lambda pb=pb, hh=hh, jj=jj, P=P: nc.scalar.activation(
                                out=P[:, jj, hh * 512:(hh + 1) * 512], in_=pb[:], func=AF.Exp),
                                reads=[pr], writes=[pt_res[pi][jj]])

                def spv(ku):
                    m, hg = ku // 2, ku % 2
                    pm = 4 * qp + m
                    nj = min(5, pm + 1)
                    Kb, kres, Vb, vres_ = band[m]
                    qs = slice(m * 128, (m + 1) * 128)
                    pi = ku % NPT
                    P = PT[pi]
                    sbk, sres = bank()
                    obk, ores = bank()
                    for jj in range(nj):
                        for hh in range(2):
                            k.op(k.pe, lambda hh=hh, jj=jj: nc.tensor.matmul(
                                sbk[hh * 64:(hh + 1) * 64, :], onesb[:, 0:64], P[:, jj, hh * 512:(hh + 1) * 512],
                                start=(jj == 0), stop=(jj == nj - 1), skip_group_check=True),
                                reads=[pt_res[pi][jj], const_res], writes=[sres])
                    for jj in range(nj):
                        for di in range(4):
                            dt = hg * 4 + di
                            for hh in range(2):
                                k.op(k.pe, lambda dt=dt, hh=hh, di=di, jj=jj: nc.tensor.matmul(
                                    obk[hh * 64:(hh + 1) * 64, di * 128:(di + 1) * 128],
                                    Vb[:, jj, dt * 128 + hh * 64:dt * 128 + (hh + 1) * 64],
                                    P[:, jj, hh * 512 + di * 128:hh * 512 + (di + 1) * 128],
                                    start=(jj == 0 and di == 0), stop=(jj == nj - 1), skip_group_check=True),
                                    reads=[vres_, pt_res[pi][jj]], writes=[ores])
                    ri = ku % 2
                    k.op(k.dve, lambda: V.reciprocal(out=rinv[ri][:], in_=sbk[:]),
                         reads=[sres], writes=[rinv_res[ri]])
                    k.op(k.dve, lambda: V.tensor_tensor(
                        out=oT[:, hg * 4:(hg + 1) * 4, qs],
                        in0=obk[:].rearrange("p (a b) -> p a b", a=4),
                        in1=rinv[ri][:].rearrange("p (a b) -> p a b", a=4), op=ALU.mult),
                        reads=[ores, rinv_res[ri]], writes=o_res[hg * 4:(hg + 1) * 4])

                qk(0)
                for ku in range(8):
                    if ku + 1 < 8:
                        qk(ku + 1)
                    spv(ku)
                    if ku < 7:
                        yield 2
                yield
                for c0 in (0, 512):
                    view, vres = wload((NDT, 512), [(0, NDT, 0, 512, wo[:, :, c0:c0 + 512])])
                    for dj in range(4):
                        dt = c0 // 128 + dj
                        pb, pr = bank()
                        for kt in range(NDT):
                            k.op(k.pe, lambda pb=pb, kt=kt, dj=dj: nc.tensor.matmul(
                                pb[:], view[:, kt, dj * 128:(dj + 1) * 128], oT[:, kt, :],
                                start=(kt == 0), stop=(kt == NDT - 1)),
                                reads=[vres, o_res[kt]], writes=[pr])
                        epilogue(ci, dt, gt, pb, pr, si)
                ln_finish(gt, si)

    def pro(ci, tts, hoff=0, scale=True):
        pieces = []
        for li, gt in enumerate(tts):
            for dt in range(NDT):
                pieces.append(lambda dt=dt, li=li, gt=gt: k.op(k.act, lambda: nc.scalar.activation(
                    out=hT[:, dt, (hoff + li) * TT:(hoff + li + 1) * TT], in_=Nt[:, dt, gt * TT:(gt + 1) * TT],
                    func=AF.Identity, scale=coef[:, ci, 0, dt:dt + 1], bias=coef[:, ci, 1, dt:dt + 1]),
                    reads=[Nres[dt][gt], coef_res[ci]], writes=[hres[dt][hoff + li]]))
        if scale:
            for gt in tts:
                for dt in range(NDT):
                    pieces.append(lambda dt=dt, gt=gt: k.op(k.act, lambda: nc.scalar.activation(
                        out=Nt[:, dt, gt * TT:(gt + 1) * TT], in_=Nt[:, dt, gt * TT:(gt + 1) * TT],
                        func=AF.Identity, scale=coef[:, ci, 2, dt:dt + 1], bias=coef[:, ci, 3, dt:dt + 1]),
                        reads=[coef_res[ci]], writes=[Nres[dt][gt]]))
        return pieces

    def ln_affine(l, i):
        o, _ = PK_OFF["ln"]
        base = o + ((l * 3 + i) * 2) * NDT
        return pkt[:, base:base + NDT], pkt[:, base + NDT:base + 2 * NDT], pk_res

    def prev_affine(l, i):
        if l == 0 and i == 0:
            return gb0[:, 0, :], gb0[:, 1, :], gb0_res
        return ln_affine(l, i - 1) if i > 0 else ln_affine(l - 1, 2)

    def coefs(l, ii):
        for i in ii:
            g_, b_, r_ = prev_affine(l, i)
            make_coef(3 * l + i, g_, b_, r_, 3 * i, 3 * i + 1, 3 * i + 2, 0.5 if i != 1 else 1.0)

    def layer_mod_hooks(l, split=None):
        return mod_chunks(w_ada[l], 9 * D, pkcol("b_ada")[:, l * 72:(l + 1) * 72], modT, mod_res, 0, split=split)

    def kv_mod_hooks():
        return mod_chunks(w_ada_kv, 2 * D, pkcol("b_ada_kv"), modkv, modkv_res, 72)

    plan = []
    for l in range(n_layers):
        for i in range(3):
            ci = 3 * l + i
            pre = None
            post = None
            if i == 0 and l > 0:
                pre = (lambda l=l: (flush_hooks(), coefs(l, (0, 1, 2))))
            if l == 0 and i == 1:
                pre = (lambda: (flush_hooks(), coefs(0, (1, 2))))
            if i != 1:
                pros = [pro(ci, [0, 1]), pro(ci, [2, 3])]
                gen = (lambda ci=ci, l=l, i=i: ffn(ci, l, 0 if i == 0 else 1))
            elif l < N_A:
                pros = [pro(ci, [p], hoff=p % 2) for p in range(4)]
                gen = (lambda ci=ci, l=l: gmlp(ci, l))
            else:
                pros = [pro(ci, [p], hoff=p % 2) for p in range(4)]
                gen = (lambda ci=ci, l=l: attention(ci, l))
            start = None
            if i == 0 and l >= N_A:
                def start(l=l):
                    hooks.extend(bias_hooks(l))
            if i == 1 and l + 1 < n_layers:
                def start(l=l):
                    if l == N_A - 1:
                        hooks.extend(kv_mod_hooks())
                    hooks.extend(layer_mod_hooks(l + 1))
            plan.append([pre, pros, gen, start])
        if l == N_A - 1 and n_layers > N_A:
            def kvpre(l=l):
                flush_hooks()
                g_, b_, r_ = ln_affine(l, 2)
                make_coef(3 * DEPTH, g_, b_, r_, 0, 1, None, None, msrc=modkv, msrc_res=modkv_res)
            plan.append([kvpre, [pro(3 * DEPTH, [0, 1], scale=False), pro(3 * DEPTH, [2, 3], scale=False)],
                         (lambda: kv_produce(3 * DEPTH)), None])

    m0 = layer_mod_hooks(0, split=6)
    for f in m0[:7]:
        f()
    hooks.extend(m0[7:])
    load_x([2, 3])
    coefs(0, (0,))

    ov = outT.rearrange("(dt p) t -> p dt t", p=128)
    s_out = k.dsrc("s_out")
    ctmp2 = sb("ctmp2", [128, 2, NDT])
    c2_res = Res()
    gfin, bfin, gbfin_res = ln_affine(n_layers - 1, 2)
    k.op(k.dve, lambda: nc.vector.tensor_copy(out=ctmp2[:, 0, :], in_=gfin), reads=[gbfin_res], writes=[c2_res])
    k.op(k.dve, lambda: nc.vector.tensor_copy(out=ctmp2[:, 1, :], in_=bfin), reads=[gbfin_res, c2_res], writes=[c2_res])

    def out_stage(gts):
        for gt in gts:
            for dt in range(NDT):
                nsl = Nt[:, dt, gt * TT:(gt + 1) * TT]
                k.op(k.act, lambda nsl=nsl, dt=dt: nc.scalar.activation(
                    out=nsl, in_=nsl, func=AF.Identity, scale=ctmp2[:, 0, dt:dt + 1], bias=ctmp2[:, 1, dt:dt + 1]),
                    reads=[c2_res], writes=[Nres[dt][gt]])
                k.dma(k.sp, s_out, ov[:, dt, gt * TT:(gt + 1) * TT], nsl, reads=[Nres[dt][gt]])

    flat = []
    for ui, (pre, pros, gen, start) in enumerate(plan):
        for pi_, pf in enumerate(pros):
            flat.append([pre if pi_ == 0 else None, list(pf), False])
    fstate = {"i": 0}

    def emit_next_prologue(nparts=None):
        fi = fstate["i"]
        if fi >= len(flat):
            if nparts is None and not fstate.get("out01"):
                fstate["out01"] = True
                flush_deferred()
                out_stage([0, 1])
            return
        ent = flat[fi]
        if not ent[2]:
            flush_deferred()
            if ent[0] is not None:
                ent[0]()
            ent[2] = True
        n = len(ent[1]) if nparts is None else min(nparts, len(ent[1]))
        for _ in range(n):
            ent[1].pop(0)()
        if nparts is None:
            fstate["i"] = fi + 1

    emit_next_prologue()
    for (pre, pros, gen, start) in plan:
        if start is not None:
            start()
        for y in gen():
            emit_next_prologue(y)
    flush_hooks()
    flush_deferred()
    gprev, bprev, gb_res = ln_affine(n_layers - 1, 2)

    out_stage([2, 3] if fstate.get("out01") else [0, 1, 2, 3])
    k.sp.h.wait_ge(s_out.sem, s_out.n)
    es.close()
    return nc


PK_OFF = {}
_o = 0
for _name, _n in (("c", NDT), ("b_ada", DEPTH * 72), ("ln", DEPTH * 3 * 2 * NDT), ("b_in_u", N_A * 16),
                  ("b_ada_kv", 16)):
    PK_OFF[_name] = (_o, _n)
    _o += _n
PK_COLS = _o


def _fm(v):
    v = np.asarray(v, np.float32)
    lead = v.shape[:-1]
    n = v.shape[-1] // 128
    a = v.reshape(lead + (n, 128))
    a = np.moveaxis(a, -1, 0)
    return np.ascontiguousarray(a.reshape(128, -1))


def make_in_maps(inputs):
    f = lambda a: np.ascontiguousarray(np.asarray(a, np.float32))
    x = f(inputs["x"])
    c = f(inputs["c"])
    b_ada = f(inputs["b_ada"])
    ln = np.stack([f(inputs["ln_g"]), f(inputs["ln_b"])], axis=2)
    b_in = f(inputs["gmlp_b_in"])
    rel = f(inputs["attn_rel_bias"])
    j = np.arange(5)[:, None, None]
    r = np.arange(128)[None, :, None]
    t = np.arange(128)[None, None, :]
    dist = 128 * (4 - j) + t - r
    idx = np.clip(dist, -(HD - 1), 4 * 64) + (HD - 1)
    biasT = np.ascontiguousarray(rel[:, :, idx])
    gm_rows = np.zeros((N_A, 5, GH), np.float32)
    gm_rows[:, 4] = 1.0
    gm_rows[:, 0] = b_in[:, GH:]
    gm_rows[:, 1] = f(inputs["gmlp_ln_g"])
    gm_rows[:, 2] = f(inputs["gmlp_ln_b"])
    gm_rows[:, 3, :1024] = f(inputs["gmlp_b_s"]).reshape(N_A, 1024)
    shared = {
        "w_ada": f(inputs["w_ada"]), "ffn_gu": f(inputs["ffn_gu"]), "ffn_down": f(inputs["ffn_down"]),
        "gmlp_w_in": f(inputs["gmlp_w_in"]), "gmlp_w_out": f(inputs["gmlp_w_out"]),
        "gmlp_w_sT": np.ascontiguousarray(f(inputs["gmlp_w_s"]).transpose(0, 3, 1, 2).reshape(N_A, 128, 1024)),
        "gm_rows": gm_rows,
        "w_ada_kv": f(inputs["w_ada_kv"]), "w_kv": f(inputs["w_kv"]),
        "attn_w_q": f(inputs["attn_w_q"]), "attn_w_o": f(inputs["attn_w_o"]), "biasT": biasT,
        "ident": np.eye(128, dtype=np.float32),
    }
    maps = []
    for b in range(NB):
        pkb = np.concatenate([
            _fm(c[b]), _fm(b_ada), _fm(ln), _fm(b_in[:, :GH]), _fm(f(inputs["b_ada_kv"]))], axis=1)
        assert pkb.shape == (128, PK_COLS), pkb.shape
        m = dict(shared)
        m["xT"] = np.ascontiguousarray(x[b].T)
        m["pk"] = np.ascontiguousarray(pkb)
        maps.append(m)
    return maps


_NC_CACHE = {}


def run(inputs, n_layers=DEPTH, dbg=None, trace=False):
    key = (n_layers, dbg)
    if key not in _NC_CACHE:
        _NC_CACHE[key] = build_program(n_layers, dbg)
    nc = _NC_CACHE[key]
    maps = make_in_maps(inputs)
    res = run_bass_kernel_spmd(nc, maps, core_ids=list(range(NB)), trace=trace)
    out = np.stack([np.ascontiguousarray(r["outT"].T) for r in res.results], axis=0)
    return out.astype(np.float32), res


def kernel(**inputs):
    out, _ = run(inputs)
    return out
```
